# Optimizing a Trainium2 kernel written in Bass

```python
import math
import jax, jax.numpy as jnp
from jax import lax
import numpy as np

D_MODEL = 1024
BATCH = 16
SEQ = 2048
DEPTH = 2

N_A = DEPTH // 2
N_B = DEPTH - N_A

H_A = 16
NOPE_A = 64
ROPE_A = 32
V_A = 64
Q_LORA = 768
KV_LORA = 256
WIDTH_A = H_A * V_A
QBLOCK = 128

H_B = 16
HD_B = 64
WIDTH_B = H_B * HD_B
MOBA_BLOCK = 256
MOBA_TOPK = 3
QCHUNK = 8

THETA = 10000.0
LN_EPS = 1e-5
RMS_EPS = 1e-6
ALPHA = (2 * DEPTH) ** 0.25
BETA = (8 * DEPTH) ** (-0.25)

kernel_name = "yoco_mla_moba_gated_deepnorm"


def rope_tables(seq, dim):
    inv = THETA ** (-jnp.arange(0, dim, 2, dtype=jnp.float32) / dim)
    ang = jnp.arange(seq, dtype=jnp.float32)[:, None] * inv[None, :]
    ang = jnp.concatenate([ang, ang], axis=-1)
    return jnp.cos(ang), jnp.sin(ang)


def apply_rope(x, cos, sin):
    x1, x2 = jnp.split(x, 2, axis=-1)
    rot = jnp.concatenate([-x2, x1], axis=-1)
    return (x * cos + rot * sin).astype(x.dtype)


def layer_norm(x, g, b):
    xf = x.astype(jnp.float32)
    mu = jnp.mean(xf, axis=-1, keepdims=True)
    var = jnp.mean(jnp.square(xf - mu), axis=-1, keepdims=True)
    return ((xf - mu) * lax.rsqrt(var + LN_EPS) * g + b).astype(x.dtype)


def rms_norm(x, g):
    xf = x.astype(jnp.float32)
    return (xf * lax.rsqrt(jnp.mean(jnp.square(xf), axis=-1, keepdims=True) + RMS_EPS) * g).astype(x.dtype)


def mla_mixer(x, w_in, q_norm, kv_norm, w_uq, w_ukv, w_o, cos, sin):
    B, S, _ = x.shape
    h = x @ w_in
    c_q, c_kv, k_rope, gate = jnp.split(h, [Q_LORA, Q_LORA + KV_LORA, Q_LORA + KV_LORA + ROPE_A], axis=-1)
    q = (rms_norm(c_q, q_norm) @ w_uq).reshape(B, S, H_A, NOPE_A + ROPE_A)
    q_nope = q[..., :NOPE_A]
    q_rope = apply_rope(q[..., NOPE_A:], cos[None, :, None, :], sin[None, :, None, :])
    kv = (rms_norm(c_kv, kv_norm) @ w_ukv).reshape(B, S, H_A, NOPE_A + V_A)
    k_nope, v = kv[..., :NOPE_A], kv[..., NOPE_A:]
    k_rope = apply_rope(k_rope, cos[None], sin[None])
    scale = (NOPE_A + ROPE_A) ** -0.5
    outs = []
    for i in range(S // QBLOCK):
        q0, kend = i * QBLOCK, (i + 1) * QBLOCK
        s = (jnp.einsum('bqhd,bkhd->bhqk', q_nope[:, q0:kend], k_nope[:, :kend])
             + jnp.einsum('bqhr,bkr->bhqk', q_rope[:, q0:kend], k_rope[:, :kend])).astype(jnp.float32) * scale
        q_pos = q0 + jnp.arange(QBLOCK)
        mask = jnp.arange(kend)[None, :] <= q_pos[:, None]
        p = jax.nn.softmax(jnp.where(mask, s, -jnp.inf), axis=-1).astype(v.dtype)
        outs.append(jnp.einsum('bhqk,bkhd->bqhd', p, v[:, :kend]))
    o = jnp.concatenate(outs, axis=1).reshape(B, S, WIDTH_A)
    return (o * jax.nn.silu(gate)) @ w_o


def moba_shared_kv(x, w_kv, cos, sin):
    B, S, _ = x.shape
    kv = x @ w_kv
    k = kv[..., :WIDTH_B].reshape(B, S, H_B, HD_B)
    v = kv[..., WIDTH_B:].reshape(B, S, H_B, HD_B)
    k = apply_rope(k, cos[None, :, None, :], sin[None, :, None, :])
    nb = -(-S // MOBA_BLOCK)
    pad = nb * MOBA_BLOCK - S
    k = jnp.pad(k, ((0, 0), (0, pad), (0, 0), (0, 0)))
    v = jnp.pad(v, ((0, 0), (0, pad), (0, 0), (0, 0)))
    k_blocks = k.reshape(B, nb, MOBA_BLOCK, H_B, HD_B).transpose(0, 3, 1, 2, 4)
    v_blocks = v.reshape(B, nb, MOBA_BLOCK, H_B, HD_B).transpose(0, 3, 1, 2, 4)
    cnt = jnp.clip(S - jnp.arange(nb) * MOBA_BLOCK, 1, MOBA_BLOCK).astype(jnp.float32)
    k_mean = (jnp.sum(k_blocks.astype(jnp.float32), axis=3) / cnt[:, None]).astype(k.dtype)
    return k_blocks, v_blocks, k_mean


def moba_mixer(x, k_blocks, v_blocks, k_mean, w_in, w_o, cos, sin):
    B, S, _ = x.shape
    nb = k_blocks.shape[2]
    h = x @ w_in
    q, gate = h[..., :WIDTH_B], h[..., WIDTH_B:]
    q = apply_rope(q.reshape(B, S, H_B, HD_B), cos[None, :, None, :], sin[None, :, None, :])
    q = q.transpose(0, 2, 1, 3)
    gs = jnp.einsum('bhsd,bhnd->bhsn', q, k_mean).astype(jnp.float32)
    q_blk = jnp.arange(S) // MOBA_BLOCK
    past = jnp.arange(nb)[None, :] < q_blk[:, None]
    gs = jnp.where(past, gs, -jnp.inf)
    topk = min(MOBA_TOPK, nb)
    _, idx = lax.top_k(gs, topk)
    valid = idx < q_blk[:, None]
    n_chunks = S // QCHUNK
    q_c = q.reshape(B, H_B, n_chunks, QCHUNK, HD_B).transpose(2, 0, 1, 3, 4)
    idx_c = idx.reshape(B, H_B, n_chunks, QCHUNK, topk).transpose(2, 0, 1, 3, 4)
    valid_c = valid.reshape(B, H_B, n_chunks, QCHUNK, topk).transpose(2, 0, 1, 3, 4)
    starts = jnp.arange(n_chunks, dtype=jnp.int32) * QCHUNK
    b_ix = jnp.arange(B)[:, None, None, None]
    h_ix = jnp.arange(H_B)[None, :, None, None]
    scale = HD_B ** -0.5

    def attend(args):
        qq, ii, vv, start = args
        k_sel = k_blocks[b_ix, h_ix, ii]
        v_sel = v_blocks[b_ix, h_ix, ii]
        qb = start // MOBA_BLOCK
        k_own = lax.dynamic_index_in_dim(k_blocks, qb, axis=2, keepdims=False)
        v_own = lax.dynamic_index_in_dim(v_blocks, qb, axis=2, keepdims=False)
        s_sel = jnp.einsum('bhcd,bhctkd->bhctk', qq, k_sel).astype(jnp.float32) * scale
        s_sel = jnp.where(vv[..., None], s_sel, -jnp.inf).reshape(B, H_B, QCHUNK, topk * MOBA_BLOCK)
        s_own = jnp.einsum('bhcd,bhkd->bhck', qq, k_own).astype(jnp.float32) * scale
        k_pos = qb * MOBA_BLOCK + jnp.arange(MOBA_BLOCK)
        q_pos = start + jnp.arange(QCHUNK)
        s_own = jnp.where(k_pos[None, :] <= q_pos[:, None], s_own, -jnp.inf)
        p = jax.nn.softmax(jnp.concatenate([s_sel, s_own], axis=-1), axis=-1).astype(qq.dtype)
        p_sel = p[..., :topk * MOBA_BLOCK].reshape(B, H_B, QCHUNK, topk, MOBA_BLOCK)
        p_own = p[..., topk * MOBA_BLOCK:]
        return (jnp.einsum('bhctk,bhctkd->bhcd', p_sel, v_sel)
                + jnp.einsum('bhck,bhkd->bhcd', p_own, v_own))

    o = lax.map(attend, (q_c, idx_c, valid_c, starts))
    o = o.transpose(1, 0, 3, 2, 4).reshape(B, S, WIDTH_B)
    return (o * jax.nn.silu(gate)) @ w_o


def setup_inputs(seed: int = 0) -> dict:
    key = jax.random.key(seed)
    ks = jax.random.split(key, 14)

    def nrm(k, shape, fan_in, scale=1.0):
        return jax.random.normal(k, shape, jnp.float32) * (scale * fan_in ** -0.5)

    x = jax.random.normal(ks[0], (BATCH, SEQ, D_MODEL), jnp.float32)
    mla_w_in = nrm(ks[1], (N_A, D_MODEL, Q_LORA + KV_LORA + ROPE_A + WIDTH_A), D_MODEL)
    mla_q_norm = 1.0 + 0.02 * jax.random.normal(ks[2], (N_A, Q_LORA), jnp.float32)
    mla_kv_norm = 1.0 + 0.02 * jax.random.normal(ks[3], (N_A, KV_LORA), jnp.float32)
    mla_w_uq = nrm(ks[4], (N_A, Q_LORA, H_A * (NOPE_A + ROPE_A)), Q_LORA)
    mla_w_ukv = nrm(ks[5], (N_A, KV_LORA, H_A * (NOPE_A + V_A)), KV_LORA)
    mla_w_o = nrm(ks[6], (N_A, WIDTH_A, D_MODEL), WIDTH_A, BETA)
    moba_w_kv = nrm(ks[7], (D_MODEL, 2 * WIDTH_B), D_MODEL)
    moba_w_in = nrm(ks[8], (N_B, D_MODEL, 2 * WIDTH_B), D_MODEL)
    moba_w_o = nrm(ks[9], (N_B, WIDTH_B, D_MODEL), WIDTH_B, BETA)
    ln_g = 1.0 + 0.02 * jax.random.normal(ks[10], (DEPTH, D_MODEL), jnp.float32)
    ln_b = 0.02 * jax.random.normal(ks[11], (DEPTH, D_MODEL), jnp.float32)
    return {"x": x, "mla_w_in": mla_w_in, "mla_q_norm": mla_q_norm, "mla_kv_norm": mla_kv_norm,
            "mla_w_uq": mla_w_uq, "mla_w_ukv": mla_w_ukv, "mla_w_o": mla_w_o,
            "moba_w_kv": moba_w_kv, "moba_w_in": moba_w_in, "moba_w_o": moba_w_o,
            "ln_g": ln_g, "ln_b": ln_b}


def reference(x, mla_w_in, mla_q_norm, mla_kv_norm, mla_w_uq, mla_w_ukv, mla_w_o,
              moba_w_kv, moba_w_in, moba_w_o, ln_g, ln_b):
    S = x.shape[1]
    cos_a, sin_a = rope_tables(S, ROPE_A)
    cos_b, sin_b = rope_tables(S, HD_B)
    shared = None
    for layer in range(DEPTH):
        if layer < N_A:
            y = mla_mixer(x, mla_w_in[layer], mla_q_norm[layer], mla_kv_norm[layer],
                          mla_w_uq[layer], mla_w_ukv[layer], mla_w_o[layer], cos_a, sin_a)
        else:
            if shared is None:
                shared = moba_shared_kv(x, moba_w_kv, cos_b, sin_b)
            j = layer - N_A
            y = moba_mixer(x, shared[0], shared[1], shared[2], moba_w_in[j], moba_w_o[j], cos_b, sin_b)
        x = layer_norm(ALPHA * x + y, ln_g[layer], ln_b[layer])
    return x
```

```python
import numpy as np
from contextlib import ExitStack
import concourse.bass as bass
import concourse.mybir as mybir
from concourse.bass_utils import run_bass_kernel_spmd

F32 = mybir.dt.float32
BF16 = mybir.dt.bfloat16
AF = mybir.ActivationFunctionType
ALU = mybir.AluOpType
AX = mybir.AxisListType

NCORES = 8
S = 2048
D = 1024
NT = S // 128
DEPTH = 2
ALPHA = float((2 * DEPTH) ** 0.25)
LN_EPS = 1e-5
RMS_EPS = 1e-6
Q_LORA, KV_LORA, ROPE_A = 768, 256, 32
SCALE_A = float(96 ** -0.5)
SCALE_B = float(64 ** -0.5)
NEG_BIG = -30000.0
THETA = 10000.0


class Sched:
    def __init__(self, nc, stack):
        self.nc = nc
        self.stack = stack
        self.ops = []
        self.last_w = {}
        self.readers = {}
        self.eng_obj = {"pe": nc.tensor, "act": nc.scalar, "dve": nc.vector,
                        "pool": nc.gpsimd, "sp": nc.sync}

    def add(self, eng, fn, reads=(), writes=(), dma=None):
        idx = len(self.ops)
        deps = {}
        for b in reads:
            p = self.last_w.get(b)
            if p is not None:
                deps[p] = True
            if isinstance(b, tuple) and b[0] == "ps":
                for r in self.readers.get(b, ()):
                    if r not in deps:
                        deps[r] = False
        for b in writes:
            p = self.last_w.get(b)
            if p is not None:
                deps[p] = True
            for r in self.readers.get(b, ()):
                if r not in deps:
                    deps[r] = False
        self.ops.append(dict(eng=eng, fn=fn, deps=deps, dma=dma, signal=False, cnt=None))
        for b in writes:
            self.last_w[b] = idx
            self.readers[b] = []
        for b in reads:
            if b not in writes:
                self.readers.setdefault(b, []).append(idx)
        return idx

    def emit(self, final_wait_engine="sp"):
        nc = self.nc
        ops = self.ops
        need = []
        for i, op in enumerate(ops):
            lst = []
            for p, strong in op["deps"].items():
                po = ops[p]
                if po["dma"] is not None:
                    lst.append(p)
                elif po["eng"] == op["eng"] and op["dma"] is None:
                    if op["eng"] == "pe":
                        continue
                    lst.append(p)
                else:
                    lst.append(p)
            youngest = {}
            keep = []
            for p in lst:
                po = ops[p]
                if po["dma"] is not None:
                    keep.append(p)
                else:
                    e_ = po["eng"]
                    if e_ not in youngest or p > youngest[e_]:
                        youngest[e_] = p
            keep += list(youngest.values())
            lst = keep
            need.append(lst)
            for p in lst:
                ops[p]["signal"] = True
        sem = {}
        cnt = {}

        def get_sem(key):
            if key not in sem:
                sem[key] = self.stack.enter_context(nc.semaphore("s%d" % len(sem)))
                cnt[key] = 0
            return sem[key]

        for op in ops:
            if op["dma"] is not None:
                key = ("dma", op["dma"])
                get_sem(key)
                cnt[key] += 16
                op["cnt"] = cnt[key]
                op["semkey"] = key
            elif op["signal"]:
                key = ("eng", op["eng"])
                get_sem(key)
                cnt[key] += 1
                op["cnt"] = cnt[key]
                op["semkey"] = key
        self.n_sems = len(sem)
        self.max_cnt = dict(cnt)
        seen = {e: {} for e in self.eng_obj}
        done_vc = {}
        self.n_waits = 0
        for i, op in enumerate(ops):
            e = op["eng"]
            eo = self.eng_obj[e]
            waits = {}
            for p in need[i]:
                po = ops[p]
                k = po["semkey"]
                waits[k] = max(waits.get(k, 0), po["cnt"])
            for k, v in waits.items():
                if seen[e].get(k, 0) >= v:
                    continue
                eo.wait_ge(sem[k], v)
                self.n_waits += 1
                seen[e][k] = v
                for k2, v2 in done_vc.get((k, v), {}).items():
                    if seen[e].get(k2, 0) < v2:
                        seen[e][k2] = v2
            ins = op["fn"](eo)
            if op["dma"] is not None:
                ins.then_inc(sem[op["semkey"]], 16)
                vc = dict(seen[e])
                vc[op["semkey"]] = op["cnt"]
                done_vc[(op["semkey"], op["cnt"])] = vc
            elif op["signal"]:
                ins.then_inc(sem[op["semkey"]], 1)
                vc = dict(seen[e])
                vc[op["semkey"]] = op["cnt"]
                done_vc[(op["semkey"], op["cnt"])] = vc
        eo = self.eng_obj[final_wait_engine]
        for k, v in cnt.items():
            if k[0] == "dma" and seen[final_wait_engine].get(k, 0) < v:
                eo.wait_ge(sem[k], v)


class Rot:
    def __init__(self, items):
        self.items = list(items)
        self.i = 0

    def next(self):
        v = self.items[self.i % len(self.items)]
        self.i += 1
        return v


def build_program(nseq=2, layers=(0, 1), dbg_x1=False, max_ops=None):
    nc = bass.Bass("TRN2", target_bir_lowering=False)

    def din(name, shape):
        return nc.dram_tensor(name, list(shape), F32, kind="ExternalInput").ap()

    x_d = din("x", [nseq, S, D])
    w_in0_d = din("w_in0", [D, 2080])
    gains_d = din("gains", [128, 8])
    w_uq_d = din("w_uq", [Q_LORA, 1536])
    w_ukv_d = din("w_ukv", [KV_LORA, 2048])
    w_o0_d = din("w_o0", [D, D])
    w_kv_d = din("w_kv", [D, 2048])
    w_in1_d = din("w_in1", [D, 2048])
    w_o1_d = din("w_o1", [D, D])
    ln_g_d = din("ln_g", [2, D])
    ln_b_d = din("ln_b", [2, D])
    ident_d = din("ident", [128, 128])
    tri_d = din("tri", [128, 128])
    cosA_d = din("cosA", [128, NT, 32])
    ssA_d = din("ssA", [128, NT, 32])
    cosB_d = din("cosB", [128, NT, 64])
    ssB_d = din("ssB", [128, NT, 64])
    onehot_d = din("onehot", [8, S])
    biasc_d = din("biasc", [8, 1024])
    out_d = nc.dram_tensor("out", [nseq, S, D], F32, kind="ExternalOutput").ap()
    if dbg_x1:
        x1_d = nc.dram_tensor("x1dbg", [nseq, S, D], F32, kind="ExternalOutput").ap()
    else:
        x1_d = nc.dram_tensor("x1scr", [nseq, S, D], F32).ap()

    with ExitStack() as st:
        def sb(name, shape, dt):
            return st.enter_context(nc.sbuf_tensor("sb_" + name, list(shape), dt))

        B1 = sb("B1", [128, 8, S], BF16)
        B2 = sb("B2", [128, 8, S], BF16)
        B3 = sb("B3", [128, 8, S], BF16)
        QK = sb("QK", [128, 8192], BF16)
        VG = sb("VG", [128, NT, 3, 64], BF16)
        W1 = sb("W1", [128, 13312], BF16)
        gains = sb("gains", [128, 8], F32)
        ident = sb("ident", [128, 128], BF16)
        tri = sb("tri", [128, 128], BF16)
        ones_bf = sb("ones_bf", [128, 8], BF16)
        mhalf = sb("mhalf", [128, 32], F32)
        tabC = sb("tabC", [128, NT * 64], F32)
        tabS = sb("tabS", [128, NT * 64], F32)
        cosA = tabC[:, 0:NT * 32].rearrange("p (t d) -> p t d", t=NT)
        ssA = tabS[:, 0:NT * 32].rearrange("p (t d) -> p t d", t=NT)
        cosB = tabC[:, :].rearrange("p (t d) -> p t d", t=NT)
        ssB = tabS[:, :].rearrange("p (t d) -> p t d", t=NT)
        g_bc = sb("g_bc", [128, D], F32)
        b_bc = sb("b_bc", [128, D], F32)
        xb = [sb("xb%d" % i, [128, D], BF16) for i in range(2)]
        xres = [sb("xres%d" % i, [128, D], F32) for i in range(4)]
        sq = [sb("sq%d" % i, [128, 512], BF16) for i in range(2)]
        PT = [sb("PT%d" % i, [128, 1024], BF16) for i in range(3)]
        q32 = [sb("q32_%d" % i, [128, 256], F32) for i in range(2)]
        kv32 = [sb("kv32_%d" % i, [128, 256], F32) for i in range(2)]
        Qa = [sb("Qa%d" % i, [128, 2, 96], BF16) for i in range(3)]
        Ka = [sb("Ka%d" % i, [128, 2, 96], BF16) for i in range(3)]
        rt1 = [sb("rt1_%d" % i, [128, 2, 64], F32) for i in range(2)]
        rt2 = [sb("rt2_%d" % i, [128, 2, 64], F32) for i in range(2)]
        krope = sb("krope", [128, NT, 32], BF16)
        wkr = sb("wkr", [128, 8, 32], BF16)
        rstd = sb("rstd", [128, 32], F32)
        rg = [sb("rg%d" % i, [128, 512], F32) for i in range(2)]
        kr1 = rg[0][:, :].rearrange("p (t d) -> p t d", t=NT)
        kr2 = rg[1][:, :].rearrange("p (t d) -> p t d", t=NT)
        gsv = [sb("gsv%d" % i, [128, 2, 8], F32) for i in range(2)]
        cmpt = [sb("cmp%d" % i, [128, 2, 8, 8], F32) for i in range(2)]
        rank = [sb("rank%d" % i, [128, 2, 8], F32) for i in range(2)]
        bias = [sb("bias%d" % i, [128, 2, 8], BF16) for i in range(4)]
        km = sb("km", [64, 2, 8], F32)
        kmT = sb("kmT", [64, 2, 8], BF16)
        lnst = [sb("lnst%d" % i, [128, 2, 6], F32) for i in range(4)]
        lnmv = [sb("lnmv%d" % i, [128, 2], F32) for i in range(4)]
        lnr = [sb("lnr%d" % i, [128, 2], F32) for i in range(4)]
        PS = st.enter_context(nc.psum_tensor("PS", [128, 8, 512], F32))
        ps = [PS[:, i, :] for i in range(8)]

        def psb(i):
            return ps[i].bitcast(BF16)

        qT = QK[0:96, 0:4096].rearrange("p (h s) -> p h s", h=2)
        kT = QK[0:96, 4096:8192].rearrange("p (h s) -> p h s", h=2)
        w2 = [QK[:, i * 4096:(i + 1) * 4096].rearrange("p (k n) -> p k n", k=8) for i in range(2)]
        W1a = W1[:, 0:9216]
        W1b = W1[:, 9216:13312]
        w_uq_sb = W1a.rearrange("p (f n) -> p f n", f=6)
        w_ukv_sb = W1b.rearrange("p (f n) -> p f n", f=2)
        w_o_sb = W1[:, 0:8192].rearrange("p (k n) -> p k n", k=8)
        wg = [W1[:, 0:3072].rearrange("p (k n) -> p k n", k=8),
              W1[:, 9216:12288].rearrange("p (k n) -> p k n", k=8)]

        SC = Sched(nc, st)
        MARKS = {}

        def mark(name):
            MARKS[name] = len(SC.ops)

        def op(eng, meth, *args, reads=(), writes=(), dma=None, **kw):
            SC.add(eng, (lambda e: getattr(e, meth)(*args, **kw)), reads=reads, writes=writes, dma=dma)

        def kB(name, cs, ts):
            return [(name, c, t) for c in cs for t in ts]

        def kB1(cs, ts):
            return [("B1", c, t, hl) for c in cs for t in ts for hl in range(2)]
        R8 = range(8)
        RT = range(NT)
        K_QT = [("qT", h, t) for h in range(2) for t in RT]
        K_KT = [("kT", h, t) for h in range(2) for t in RT]
        K_W2 = [K_QT, K_KT]
        K_W1A = [("w1", 0)]
        K_W1B = [("w1", 1)]

        def kps(i):
            return [("ps", i)]

        def wT(w_d):
            return w_d.rearrange("(kc p) n -> p kc n", p=128)

        def tslice(t):
            return slice(t * 128, (t + 1) * 128)

        op("pool", "dma_start", out=ident[:], in_=ident_d, writes=["ident"], dma="ident")
        op("pool", "dma_start", out=tri[:], in_=tri_d, writes=["tri"], dma="tri")
        op("sp", "dma_start", out=gains[:], in_=gains_d, writes=["gains"], dma="gains")
        op("dve", "memset", ones_bf[:], 1.0, writes=["ones_bf"])
        op("dve", "memset", mhalf[:], -0.5, writes=["mhalf"])
        op("dve", "memset", VG[:, :, 1, :], 1.0, writes=["VG1"])

        gen = Rot([6, 7])

        def pipeline(stages, n):
            for s_ in range(n + len(stages) - 1):
                for k_ in range(len(stages) - 1, -1, -1):
                    t_ = s_ - k_
                    if 0 <= t_ < n:
                        stages[k_](t_)
        pt_rot = Rot([0, 1, 2])
        rg_rot = Rot([0, 1])

        def preload_c(layer, w_o_d):
            op("pool", "dma_start", out=w_o_sb, in_=wT(w_o_d), writes=K_W1A, dma="w1a")
            load_ln(layer)

        def load_ln(layer):
            op("sp", "dma_start", out=g_bc[:], in_=ln_g_d[layer:layer + 1, :].to_broadcast([128, D]),
               writes=["g_bc"], dma="g_bc")
            op("sp", "dma_start", out=b_bc[:], in_=ln_b_d[layer:layer + 1, :].to_broadcast([128, D]),
               writes=["b_bc"], dma="b_bc")

        def transposes_to(src_bf, srckey, DST, dkeys, t, evac_eng, bank=None):
            b = gen.next() if bank is None else bank
            pv = psb(b)
            for kc in range(8):
                op("pe", "transpose", pv[:, kc * 128:(kc + 1) * 128], src_bf[:, kc * 128:(kc + 1) * 128], ident[:],
                   reads=[srckey, "ident"], writes=kps(b))
            src = pv[:, 0:1024].rearrange("p (c n) -> p c n", c=8)
            if evac_eng == "dve":
                op("dve", "tensor_copy", DST[:, :, tslice(t)], src, reads=kps(b), writes=dkeys)
            else:
                op("act", "copy", DST[:, :, tslice(t)], src, reads=kps(b), writes=dkeys)

        def attention_group(g, Kd, scale, L=2):
            slot = Rot([0, 1])
            otb = Rot([4, 5])
            units = []
            for hl in range(2):
                for qc in range(4):
                    ob = otb.next()
                    nj = 4 * qc + 4
                    for j in range(0, 4 * qc, 2):
                        units.append(dict(hl=hl, qc=qc, js=[j, j + 1], nj=nj, ob=ob, diag=False))
                    for j in range(4 * qc, nj):
                        units.append(dict(hl=hl, qc=qc, js=[j], nj=nj, ob=ob, diag=True))

            def front(u):
                hl, qc = u["hl"], u["qc"]
                sl = slot.next()
                pi = pt_rot.next()
                if not u["diag"]:
                    qlo, n = qc * 512, 512
                    tq = list(range(qc * 4, qc * 4 + 4))
                    for k, j in enumerate(u["js"]):
                        bank = 2 * sl + k
                        op("pe", "matmul", ps[bank][:, 0:512], kT[0:Kd, hl, j * 128:(j + 1) * 128],
                           qT[0:Kd, hl, qlo:qlo + 512], start=True, stop=True,
                           reads=[("kT", hl, j)] + [("qT", hl, t) for t in tq], writes=kps(bank))
                    op("act", "activation", PT[pi][:, :].rearrange("p (b n) -> p b n", b=2),
                       PS[:, 2 * sl:2 * sl + 2, :], AF.Exp, scale=scale,
                       reads=kps(2 * sl) + kps(2 * sl + 1), writes=[("PT", pi)])
                else:
                    j = u["js"][0]
                    qlo = j * 128
                    n = (qc + 1) * 512 - qlo
                    bank = 2 * sl
                    tq = list(range(j, (qc + 1) * 4))
                    op("pe", "matmul", ps[bank][:, 0:n], kT[0:Kd, hl, j * 128:(j + 1) * 128],
                       qT[0:Kd, hl, qlo:qlo + n], start=True, stop=True,
                       reads=[("kT", hl, j)] + [("qT", hl, t) for t in tq], writes=kps(bank))
                    op("act", "activation", PT[pi][:, 0:n], ps[bank][:, 0:n], AF.Exp, scale=scale,
                       reads=kps(bank), writes=[("PT", pi)])
                    op("pool", "tensor_tensor", PT[pi][:, 0:128], PT[pi][:, 0:128], tri[:], ALU.mult,
                       reads=[("PT", pi), "tri"], writes=[("PT", pi)])
                u.update(pi=pi, qlo=qlo, n=n)

            def back(u):
                hl, qc, nj, ob, pi, n, qlo = (u[k] for k in ("hl", "qc", "nj", "ob", "pi", "n", "qlo"))
                o0 = qlo - qc * 512
                for k, j in enumerate(u["js"]):
                    vT = VG[:, j, hl:hl + 2, :].rearrange("p a d -> p (a d)")
                    op("pe", "matmul", ps[ob][:, o0:512], vT, PT[pi][:, k * 512:k * 512 + n],
                       start=(j == 0), stop=(j == nj - 1),
                       reads=[("PT", pi), ("VG", hl, j), "VG1"], writes=kps(ob))
                if u["js"][-1] == nj - 1:
                    p0 = hl * 64
                    p1 = 64 - p0
                    ri = rg_rot.next()
                    cols = slice(qc * 512, (qc + 1) * 512)
                    ts4 = list(range(qc * 4, qc * 4 + 4))
                    op("dve", "reciprocal", rg[ri][p0:p0 + 64, :], ps[ob][p1:p1 + 64, :],
                       reads=kps(ob), writes=[("rg", ri)])
                    op("dve", "tensor_tensor", rg[ri][p0:p0 + 64, :], rg[ri][p0:p0 + 64, :],
                       B2[p0:p0 + 64, g, cols], ALU.mult,
                       reads=[("rg", ri)] + kB("B2", [g], ts4), writes=[("rg", ri)])
                    op("dve", "tensor_tensor", B1[p0:p0 + 64, g, cols], ps[ob][p0:p0 + 64, :],
                       rg[ri][p0:p0 + 64, :], ALU.mult,
                       reads=kps(ob) + [("rg", ri)], writes=[("B1", g, t, hl) for t in ts4])

            for i in range(len(units) + L):
                if i < len(units):
                    front(units[i])
                if i >= L:
                    back(units[i - L])

        def phase_c(s, layer, src_d, dst_d, to_B3, w_o_d, pre_fn=None, tail_stage=None):
            if pre_fn is not None:
                pre_fn()
            ybanks = Rot([0, 1, 2, 3, 4, 5])
            ybk = {}

            def c1(t):
                i = t % 4
                tok = tslice(t)
                xr = xres[i]
                kx = [("xres", i)]
                op("sp", "dma_start", out=xr[:], in_=src_d[s, tok, :],
                   reads=[("xsrc", layer, s, t)], writes=kx, dma=("xres", i))
                yb = [ybanks.next(), ybanks.next()]
                ybk[t] = yb
                for half in range(2):
                    for c in range(8):
                        op("pe", "matmul", ps[yb[half]][:, :], B1[:, c, tok],
                           w_o_sb[:, c, half * 512:(half + 1) * 512], start=(c == 0), stop=(c == 7),
                           reads=kB1([c], [t]) + K_W1A, writes=kps(yb[half]))

            def c2(t):
                i = t % 4
                xr = xres[i]
                kx = [("xres", i)]
                yb = ybk[t]
                for half in range(2):
                    hs = slice(half * 512, (half + 1) * 512)
                    op("dve", "scalar_tensor_tensor", xr[:, hs], xr[:, hs], ALPHA, ps[yb[half]][:, :],
                       ALU.mult, ALU.add, reads=kx + kps(yb[half]), writes=kx)
                    op("dve", "bn_stats", lnst[i][:, half, :], xr[:, hs], reads=kx, writes=[("lnst", i)])
                op("dve", "bn_aggr", lnmv[i][:], lnst[i][:].rearrange("p a b -> p (a b)"),
                   reads=[("lnst", i)], writes=[("lnmv", i)])
                op("dve", "tensor_scalar", lnr[i][:, 0:1], lnmv[i][:, 1:2], LN_EPS, None, ALU.add,
                   reads=[("lnmv", i)], writes=[("lnr", i)])
                op("pool", "tensor_tensor", lnr[i][:, 0:1], lnr[i][:, 0:1], mhalf[:, 0:1], ALU.pow,
                   reads=[("lnr", i), "mhalf"], writes=[("lnr", i)])

            def c3(t):
                i = t % 4
                tok = tslice(t)
                xr = xres[i]
                kx = [("xres", i)]
                op("dve", "scalar_tensor_tensor", xr[:], xr[:], lnmv[i][:, 0:1], g_bc[:], ALU.subtract, ALU.mult,
                   reads=kx + [("lnmv", i), "g_bc"], writes=kx)
                op("act", "activation", xr[:], xr[:], AF.Identity, scale=lnr[i][:, 0:1],
                   reads=kx + [("lnr", i)], writes=kx)

            def c3b(t):
                i = t % 4
                tok = tslice(t)
                xr = xres[i]
                kx = [("xres", i)]
                op("pool", "tensor_tensor", xr[:], xr[:], b_bc[:], ALU.add, reads=kx + ["b_bc"], writes=kx)
                op("sp", "dma_start", out=dst_d[s, tok, :], in_=xr[:],
                   reads=kx, writes=[("xsrc", layer + 1, s, t)], dma=("xres", i))
                if to_B3:
                    j = t % 2
                    op("act", "copy", xb[j][:], xr[:], reads=kx, writes=[("xb", j)])

            c4b = {}

            def c4(t):
                j = t % 2
                b = gen.next()
                c4b[t] = b
                pv = psb(b)
                for kc in range(8):
                    op("pe", "transpose", pv[:, kc * 128:(kc + 1) * 128], xb[j][:, kc * 128:(kc + 1) * 128], ident[:],
                       reads=[("xb", j), "ident"], writes=kps(b))

            def c5(t):
                b = c4b[t]
                pv = psb(b)
                op("act", "copy", B3[:, :, tslice(t)], pv[:, 0:1024].rearrange("p (c n) -> p c n", c=8),
                   reads=kps(b), writes=kB("B3", R8, [t]))

            stages = [c1, c2, c3, c3b] + ([c4, c5] if to_B3 else []) + ([tail_stage] if tail_stage is not None else [])
            pipeline(stages, NT)

        def rope_ops(i, src3, cos_t, ss_t, hd, dst3, dstkey, srckeys, ck, sk):
            hh = hd // 2
            op("dve", "tensor_tensor", rt1[i][:, :, 0:hd], src3, cos_t.to_broadcast([128, 2, hd]), ALU.mult,
               reads=srckeys + [ck], writes=[("rt1", i)])
            op("dve", "tensor_tensor", rt2[i][:, :, 0:hh], src3[:, :, hh:hd],
               ss_t[:, :, 0:hh].to_broadcast([128, 2, hh]), ALU.mult,
               reads=srckeys + [sk], writes=[("rt2", i)])
            op("dve", "tensor_tensor", rt2[i][:, :, hh:hd], src3[:, :, 0:hh],
               ss_t[:, :, hh:hd].to_broadcast([128, 2, hh]), ALU.mult,
               reads=srckeys + [sk], writes=[("rt2", i)])
            op("dve", "tensor_tensor", dst3, rt1[i][:, :, 0:hd], rt2[i][:, :, 0:hd], ALU.add,
               reads=[("rt1", i), ("rt2", i)], writes=[dstkey])

        l0_done = set()

        def l0_prefetch(s):
            op("pool", "dma_start", out=w2[0], in_=wT(w_in0_d)[:, :, 0:512], writes=K_W2[0], dma=("w2", 0))
            op("pool", "dma_start", out=wkr[:], in_=wT(w_in0_d)[:, :, 1024:1056], writes=["wkr"], dma="wkr")

        def l0_a0(s, t, bank=6):
            i = t % 2
            op("pool", "dma_start", out=xb[i][:], in_=x_d[s, tslice(t), :], writes=[("xb", i)], dma=("xb", i))
            transposes_to(xb[i], ("xb", i), B1, kB1(R8, [t]), t, "dve" if t % 2 == 0 else "act", bank=bank)

        def l1_prefetch():
            for sl in range(2):
                op("pool", "dma_start", out=w2[sl], in_=wT(w_in1_d)[:, :, 1024 + sl * 512:1024 + (sl + 1) * 512],
                   writes=K_W2[sl], dma=("w2", sl))

        def layer0(s):
            op("sp", "dma_start", out=cosA, in_=cosA_d, writes=["tabC"], dma="tabC")
            op("sp", "dma_start", out=ssA, in_=ssA_d, writes=["tabS"], dma="tabS")
            col0 = [0, 512, 1056, 1568]
            prefetched = s in l0_done
            if not prefetched:
                l0_prefetch(s)

            def a0(t):
                if not prefetched:
                    l0_a0(s, t)

            for t in range(4):
                a0(t)
            op("pool", "dma_start", out=w_uq_sb, in_=w_uq_d.rearrange("(f p) n -> p f n", p=128),
               writes=K_W1A, dma="w1a")
            op("pool", "dma_start", out=w_ukv_sb, in_=w_ukv_d.rearrange("(f p) n -> p f n", p=128),
               writes=K_W1B, dma="w1b")
            mark('A0_end')
            op("dve", "memset", ps[7][:, 0:32], 0.0, writes=kps(7))
            banks = Rot([0, 1, 2, 3, 4, 5])
            sqr = Rot([0, 1])
            pend = []

            def flush_stats(keep):
                while len(pend) > keep:
                    si_, c4_, f_ = pend.pop(0)
                    for tt in range(4):
                        col = (c4_ * 4 + tt) * 2 + (0 if f_ < 6 else 1)
                        op("pe", "matmul", ps[7][:, col:col + 1], sq[si_][:, tt * 128:(tt + 1) * 128],
                           ones_bf[:, 0:1], start=False, stop=False, skip_group_check=True,
                           reads=[("sq", si_), "ones_bf"], writes=kps(7))
            for sl in range(4):
                wi = sl % 2
                if sl > 0:
                    op("pool", "dma_start", out=w2[wi], in_=wT(w_in0_d)[:, :, col0[sl]:col0[sl] + 512],
                       writes=K_W2[wi], dma=("w2", wi))
                for c4 in range(4):
                    cols = slice(c4 * 512, (c4 + 1) * 512)
                    ts4 = list(range(c4 * 4, c4 * 4 + 4))
                    if sl == 0 and c4 < 3:
                        for t_ in range(4 * c4 + 4, 4 * c4 + 8):
                            a0(t_)
                    for j in range(4):
                        b = banks.next()
                        for kc in range(8):
                            op("pe", "matmul", ps[b][:, :], w2[wi][:, kc, j * 128:(j + 1) * 128], B1[:, kc, cols],
                               start=(kc == 0), stop=(kc == 7),
                               reads=K_W2[wi] + kB1([kc], ts4), writes=kps(b))
                        flush_stats(1)
                        if sl < 2:
                            f = sl * 4 + j
                            op("dve", "tensor_scalar", B3[:, f, cols], ps[b][:, :], gains[:, f:f + 1], None, ALU.mult,
                               reads=kps(b) + ["gains"], writes=kB("B3", [f], ts4))
                            si = sqr.next()
                            op("act", "activation", sq[si][:], ps[b][:, :], AF.Square,
                               reads=kps(b), writes=[("sq", si)])
                            pend.append((si, c4, f))
                        else:
                            f = (sl - 2) * 4 + j
                            op("act", "activation", B2[:, f, cols], ps[b][:, :], AF.Silu,
                               reads=kps(b), writes=kB("B2", [f], ts4))
            flush_stats(0)
            mark('A1_slabs_end')
            op("dve", "tensor_scalar", rstd[:, 0:32:2], ps[7][:, 0:32:2], 1.0 / Q_LORA, RMS_EPS, ALU.mult, ALU.add,
               reads=kps(7), writes=["rstd"])
            op("dve", "tensor_scalar", rstd[:, 1:32:2], ps[7][:, 1:32:2], 1.0 / KV_LORA, RMS_EPS, ALU.mult, ALU.add,
               reads=kps(7), writes=["rstd"])
            op("pool", "tensor_tensor", rstd[:], rstd[:], mhalf[:], ALU.pow, reads=["rstd", "mhalf"], writes=["rstd"])
            for t in RT:
                for kc in range(8):
                    op("pe", "matmul", ps[6][:, t * 32:(t + 1) * 32], B1[:, kc, tslice(t)], wkr[:, kc, :],
                       start=(kc == 0), stop=(kc == 7), reads=kB1([kc], [t]) + ["wkr"], writes=kps(6))
            p6 = ps[6][:, :].rearrange("p (t d) -> p t d", t=NT)
            op("dve", "tensor_tensor", kr1, p6, cosA, ALU.mult, reads=kps(6) + ["tabC"], writes=[("rg", 0)])
            op("dve", "tensor_tensor", kr2[:, :, 0:16], p6[:, :, 16:32], ssA[:, :, 0:16], ALU.mult,
               reads=kps(6) + ["tabS"], writes=[("rg", 1)])
            op("dve", "tensor_tensor", kr2[:, :, 16:32], p6[:, :, 0:16], ssA[:, :, 16:32], ALU.mult,
               reads=kps(6) + ["tabS"], writes=[("rg", 1)])
            op("dve", "tensor_tensor", krope[:], kr1, kr2, ALU.add, reads=[("rg", 0), ("rg", 1)], writes=["krope"])
            mark('A1_end')
            pa_rot = Rot([0, 1, 2, 3])
            pc_rot = Rot([4, 5, 6, 7])
            for g in range(8):
                st_ = {}
                sc_ = {}

                def s0(t, g=g, st_=st_):
                    tok = tslice(t)
                    b = pa_rot.next()
                    st_[t] = b
                    for f in range(6):
                        op("pe", "matmul", ps[b][:, 0:192], B3[:, f, tok], w_uq_sb[:, f, g * 192:(g + 1) * 192],
                           start=(f == 0), stop=(f == 5), reads=kB("B3", [f], [t]) + K_W1A, writes=kps(b))
                    for f in range(2):
                        op("pe", "matmul", ps[b][:, 256:512], B3[:, 6 + f, tok], w_ukv_sb[:, f, g * 256:(g + 1) * 256],
                           start=(f == 0), stop=(f == 1), reads=kB("B3", [6 + f], [t]) + K_W1B, writes=kps(b))

                def s1(t, st_=st_):
                    i = t % 2
                    b = st_[t]
                    op("act", "activation", q32[i][:, 0:192], ps[b][:, 0:192], AF.Identity,
                       scale=rstd[:, 2 * t:2 * t + 1], reads=kps(b) + ["rstd"], writes=[("q32", i)])
                    op("act", "activation", kv32[i][:, :], ps[b][:, 256:512], AF.Identity,
                       scale=rstd[:, 2 * t + 1:2 * t + 2], reads=kps(b) + ["rstd"], writes=[("kv32", i)])

                def s2(t):
                    i = t % 2
                    j = t % 3
                    hd, hh = 32, 16
                    qv = q32[i][:, 0:192].rearrange("p (h d) -> p h d", h=2)
                    kvv = kv32[i][:, :].rearrange("p (h d) -> p h d", h=2)
                    src3 = qv[:, :, 64:96]
                    op("dve", "tensor_tensor", rt1[i][:, :, 0:hd], src3, cosA[:, t:t + 1, :].to_broadcast([128, 2, hd]), ALU.mult,
                       reads=[("q32", i), "tabC"], writes=[("rt1", i)])
                    op("dve", "tensor_tensor", rt2[i][:, :, 0:hh], src3[:, :, hh:hd],
                       ssA[:, t:t + 1, 0:hh].to_broadcast([128, 2, hh]), ALU.mult,
                       reads=[("q32", i), "tabS"], writes=[("rt2", i)])
                    op("dve", "tensor_tensor", rt2[i][:, :, hh:hd], src3[:, :, 0:hh],
                       ssA[:, t:t + 1, hh:hd].to_broadcast([128, 2, hh]), ALU.mult,
                       reads=[("q32", i), "tabS"], writes=[("rt2", i)])
                    op("pool", "tensor_copy", Qa[j][:, :, 0:64], qv[:, :, 0:64], reads=[("q32", i)], writes=[("Qa", j)])
                    op("pool", "tensor_copy", Ka[j][:, :, 0:64], kvv[:, :, 0:64], reads=[("kv32", i)], writes=[("Ka", j)])
                    op("pool", "tensor_copy", Ka[j][:, :, 64:96], krope[:, t:t + 1, :].to_broadcast([128, 2, 32]),
                       reads=["krope"], writes=[("Ka", j)])
                    op("pool", "tensor_copy", VG[:, t, 0:3:2, :], kvv[:, :, 64:128],
                       reads=[("kv32", i)], writes=[("VG", 0, t), ("VG", 1, t)])

                def s3(t):
                    i = t % 2
                    j = t % 3
                    op("dve", "tensor_tensor", Qa[j][:, :, 64:96], rt1[i][:, :, 0:32], rt2[i][:, :, 0:32], ALU.add,
                       reads=[("rt1", i), ("rt2", i)], writes=[("Qa", j)])

                def s4(t, sc_=sc_):
                    j = t % 3
                    b2 = pc_rot.next()
                    sc_[t] = b2
                    pv = psb(b2)
                    for hl in range(2):
                        op("pe", "transpose", pv[0:96, hl * 128:(hl + 1) * 128], Qa[j][:, hl, :], ident[:],
                           reads=[("Qa", j), "ident"], writes=kps(b2))
                    for hl in range(2):
                        op("pe", "transpose", pv[0:96, 256 + hl * 128:256 + (hl + 1) * 128], Ka[j][:, hl, :], ident[:],
                           reads=[("Ka", j), "ident"], writes=kps(b2))

                def s5(t, sc_=sc_):
                    tok = tslice(t)
                    b2 = sc_[t]
                    pv = psb(b2)
                    dst = QK[0:96, :].rearrange("p (a s) -> p a s", a=4)[:, :, tok]
                    srcv = pv[0:96, 0:512].rearrange("p (a n) -> p a n", a=4)
                    wk = [("qT", 0, t), ("qT", 1, t), ("kT", 0, t), ("kT", 1, t)]
                    if t % 2 == 0:
                        op("act", "copy", dst, srcv, reads=kps(b2), writes=wk)
                    else:
                        op("dve", "tensor_copy", dst, srcv, reads=kps(b2), writes=wk)

                pipeline([s0, s1, s2, s3, s4, s5], NT)
                mark('L0_g%d_proj_end' % g)
                if g == 7:
                    preload_c(0, w_o0_d)
                attention_group(g, 96, SCALE_A)
                mark('L0_g%d_att_end' % g)
            phase_c(s, 0, x_d, x1_d, True, w_o0_d, pre_fn=l1_prefetch)
            mark('L0_end')

        def layer1(s):
            op("sp", "dma_start", out=cosB, in_=cosB_d, writes=["tabC"], dma="tabC")
            op("sp", "dma_start", out=ssB, in_=ssB_d, writes=["tabS"], dma="tabS")
            def load_wg(g):
                gi = g % 2
                key = K_W1A if gi == 0 else K_W1B
                op("pool", "dma_start", out=wg[gi][:, :, 0:128], in_=wT(w_in1_d)[:, :, g * 128:(g + 1) * 128],
                   writes=key, dma=("wg", gi))
                op("pool", "dma_start", out=wg[gi][:, :, 128:256], in_=wT(w_kv_d)[:, :, g * 128:(g + 1) * 128],
                   writes=key, dma=("wg", gi))
                op("pool", "dma_start", out=wg[gi][:, :, 256:384],
                   in_=wT(w_kv_d)[:, :, 1024 + g * 128:1024 + (g + 1) * 128], writes=key, dma=("wg", gi))

            load_wg(0)
            banks = Rot([0, 1, 2, 3, 4, 5])
            for sl in range(2):
                wi = sl % 2
                for c4 in range(4):
                    cols = slice(c4 * 512, (c4 + 1) * 512)
                    ts4 = list(range(c4 * 4, c4 * 4 + 4))
                    for j in range(4):
                        b = banks.next()
                        f = sl * 4 + j
                        for kc in range(8):
                            op("pe", "matmul", ps[b][:, :], w2[wi][:, kc, j * 128:(j + 1) * 128], B3[:, kc, cols],
                               start=(kc == 0), stop=(kc == 7),
                               reads=K_W2[wi] + kB("B3", [kc], ts4), writes=kps(b))
                        op("act", "activation", B2[:, f, cols], ps[b][:, :], AF.Silu,
                           reads=kps(b), writes=kB("B2", [f], ts4))
            for hl in range(2):
                op("pool", "dma_start", out=kT[64:72, hl, :], in_=onehot_d,
                   writes=[("kT", hl, t) for t in RT], dma=("oh", hl))
                op("pool", "dma_start", out=qT[64:72, hl, 0:1024], in_=biasc_d,
                   writes=[("qT", hl, t) for t in range(8)], dma=("oh", hl))

            for g in range(8):
                gi = g % 2
                wkey = K_W1A if gi == 0 else K_W1B
                if g + 1 < 8:
                    load_wg(g + 1)
                if g == 7:
                    preload_c(1, w_o1_d)
                pa_rot = Rot([0, 1, 2, 3])
                pc_rot = Rot([4, 5])
                pd_rot = Rot([6, 7])
                stk = {}
                stc = {}

                def p0_(t, gi=gi, wkey=wkey, stk=stk):
                    tok = tslice(t)
                    b = pa_rot.next()
                    stk[t] = b
                    for kc in range(8):
                        op("pe", "matmul", ps[b][:, 0:384], B3[:, kc, tok], wg[gi][:, kc, 0:384],
                           start=(kc == 0), stop=(kc == 7), reads=kB("B3", [kc], [t]) + wkey, writes=kps(b))

                def p1_(t, stk=stk):
                    i = t % 2
                    b = stk[t]
                    src4 = ps[b][:, 0:256].rearrange("p (h d) -> p h d", h=4)
                    r1 = q32[i][:, :].rearrange("p (h d) -> p h d", h=4)
                    r2 = kv32[i][:, :].rearrange("p (h d) -> p h d", h=4)
                    op("dve", "tensor_tensor", r1, src4, cosB[:, t:t + 1, :].to_broadcast([128, 4, 64]), ALU.mult,
                       reads=kps(b) + ["tabC"], writes=[("q32", i)])
                    op("dve", "tensor_tensor", r2[:, :, 0:32], src4[:, :, 32:64],
                       ssB[:, t:t + 1, 0:32].to_broadcast([128, 4, 32]), ALU.mult,
                       reads=kps(b) + ["tabS"], writes=[("kv32", i)])
                    op("dve", "tensor_tensor", r2[:, :, 32:64], src4[:, :, 0:32],
                       ssB[:, t:t + 1, 32:64].to_broadcast([128, 4, 32]), ALU.mult,
                       reads=kps(b) + ["tabS"], writes=[("kv32", i)])
                    op("act", "copy", VG[:, t, 0:3:2, :], ps[b][:, 256:384].rearrange("p (h d) -> p h d", h=2),
                       reads=kps(b), writes=[("VG", 0, t), ("VG", 1, t)])

                def p2_(t):
                    i = t % 2
                    j = t % 3
                    r1 = q32[i][:, :].rearrange("p (h d) -> p h d", h=4)
                    r2 = kv32[i][:, :].rearrange("p (h d) -> p h d", h=4)
                    op("pool", "tensor_tensor", Qa[j][:, :, 0:64], r1[:, 0:2, :], r2[:, 0:2, :], ALU.add,
                       reads=[("q32", i), ("kv32", i)], writes=[("Qa", j)])
                    op("pool", "tensor_tensor", Ka[j][:, :, 0:64], r1[:, 2:4, :], r2[:, 2:4, :], ALU.add,
                       reads=[("q32", i), ("kv32", i)], writes=[("Ka", j)])

                def p3_(t, stc=stc):
                    j = t % 3
                    b2 = pc_rot.next()
                    stc[t] = b2
                    pv = psb(b2)
                    for hl in range(2):
                        op("pe", "transpose", pv[0:64, hl * 128:(hl + 1) * 128], Qa[j][:, hl, 0:64], ident[:],
                           reads=[("Qa", j), "ident"], writes=kps(b2))
                    for hl in range(2):
                        op("pe", "transpose", pv[0:64, 256 + hl * 128:256 + (hl + 1) * 128], Ka[j][:, hl, 0:64], ident[:],
                           reads=[("Ka", j), "ident"], writes=kps(b2))

                def p4_(t, stc=stc):
                    tok = tslice(t)
                    b2 = stc[t]
                    pv = psb(b2)
                    dst = QK[0:64, :].rearrange("p (a s) -> p a s", a=4)[:, :, tok]
                    srcv = pv[0:64, 0:512].rearrange("p (a n) -> p a n", a=4)
                    wk = [("qT", 0, t), ("qT", 1, t), ("kT", 0, t), ("kT", 1, t)]
                    if t % 2 == 0:
                        op("act", "copy", dst, srcv, reads=kps(b2), writes=wk)
                    else:
                        op("dve", "tensor_copy", dst, srcv, reads=kps(b2), writes=wk)

                pipeline([p0_, p1_, p2_, p3_, p4_], NT)
                for hl in range(2):
                    op("dve", "tensor_reduce", km[:, hl, :], kT[0:64, hl, :].rearrange("p (n k) -> p n k", n=8),
                       AX.X, ALU.add, reads=[("kT", hl, t) for t in RT], writes=["km"])
                op("dve", "tensor_scalar", kmT[:], km[:], 1.0 / 256.0, None, ALU.mult, reads=["km"], writes=["kmT"])
                sg = {}
                sd = {}

                def t0_(t0, sg=sg):
                    t = 8 + t0
                    bi = t % 4
                    qb = t // 2
                    tok = tslice(t)
                    b3 = pc_rot.next()
                    sg[t] = b3
                    for hl in range(2):
                        op("pe", "matmul", ps[b3][:, hl * 8:(hl + 1) * 8], qT[0:64, hl, tok], kmT[:, hl, :],
                           start=True, stop=True, reads=[("qT", hl, t), "kmT"], writes=kps(b3))
                    op("pool", "memset", bias[bi][:], NEG_BIG, writes=[("bias", bi)])
                    op("pool", "memset", bias[bi][:, :, qb:qb + 1], 0.0, writes=[("bias", bi)])

                def t1_(t0, sg=sg):
                    t = 8 + t0
                    i = t % 2
                    bi = t % 4
                    qb = t // 2
                    b3 = sg[t]
                    op("dve", "tensor_copy", gsv[i][:], ps[b3][:, 0:16].rearrange("p (h n) -> p h n", h=2),
                       reads=kps(b3), writes=[("gsv", i)])
                    op("dve", "tensor_tensor", cmpt[i][:, :, 0:qb, 0:qb],
                       gsv[i][:, :, 0:qb].unsqueeze(2).to_broadcast([128, 2, qb, qb]),
                       gsv[i][:, :, 0:qb].unsqueeze(3).to_broadcast([128, 2, qb, qb]), ALU.is_gt,
                       reads=[("gsv", i)], writes=[("cmp", i)])
                    op("dve", "tensor_reduce", rank[i][:, :, 0:qb], cmpt[i][:, :, 0:qb, 0:qb], AX.X, ALU.add,
                       reads=[("cmp", i)], writes=[("rank", i)])
                    op("dve", "tensor_scalar", bias[bi][:, :, 0:qb], rank[i][:, :, 0:qb], 3.0, NEG_BIG,
                       ALU.is_ge, ALU.mult, reads=[("rank", i)], writes=[("bias", bi)])

                def t2_(t0, sd=sd):
                    t = 8 + t0
                    bi = t % 4
                    b4 = pd_rot.next()
                    sd[t] = b4
                    pv4 = psb(b4)
                    for hl in range(2):
                        op("pe", "transpose", pv4[0:8, hl * 128:(hl + 1) * 128], bias[bi][:, hl, :], ident[:],
                           reads=[("bias", bi), "ident"], writes=kps(b4))

                def t3_(t0, sd=sd):
                    t = 8 + t0
                    tok = tslice(t)
                    pv4 = psb(sd[t])
                    op("act", "copy", qT[64:72, :, tok], pv4[0:8, 0:256].rearrange("p (h n) -> p h n", h=2),
                       reads=kps(sd[t]), writes=[("qT", 0, t), ("qT", 1, t)])

                pipeline([t0_, t1_, t2_, t3_], 8)
                attention_group(g, 72, SCALE_B)
            if s + 1 < nseq:
                l0_done.add(s + 1)
                phase_c(s, 1, x1_d, out_d, False, w_o1_d, pre_fn=(lambda: l0_prefetch(s + 1)),
                        tail_stage=(lambda t: l0_a0(s + 1, t, bank=6 + (t % 2))))
            else:
                phase_c(s, 1, x1_d, out_d, False, w_o1_d)

        for s in range(nseq):
            if 0 in layers:
                layer0(s)
            if 1 in layers:
                layer1(s)
        build_program.marks = dict(MARKS)
        if max_ops is not None:
            SC.ops = SC.ops[:max_ops]
        SC.emit()
        build_program.info = dict(n_ops=len(SC.ops), n_sems=SC.n_sems, max_cnt=SC.max_cnt, n_waits=SC.n_waits)
    return nc


def host_consts():
    pos = np.arange(S, dtype=np.float32)

    def tables(dim):
        inv = (np.float32(THETA) ** (-np.arange(0, dim, 2, dtype=np.float32) / np.float32(dim))).astype(np.float32)
        ang = (pos[:, None] * inv[None, :]).astype(np.float32)
        ang = np.concatenate([ang, ang], axis=-1)
        cos = np.cos(ang).astype(np.float32)
        sin = np.sin(ang).astype(np.float32)
        ss = sin.copy()
        ss[:, :dim // 2] = -ss[:, :dim // 2]
        tok = lambda a: np.ascontiguousarray(a.reshape(NT, 128, dim).transpose(1, 0, 2))
        return tok(cos), tok(ss)

    cosA, ssA = tables(32)
    cosB, ssB = tables(64)
    ident = np.eye(128, dtype=np.float32)
    kk = np.arange(128)
    tri = (kk[None, :] >= kk[:, None]).astype(np.float32)
    onehot = np.zeros((8, S), np.float32)
    for n in range(8):
        onehot[n, n * 256:(n + 1) * 256] = 1.0
    biasc = np.full((8, 1024), NEG_BIG, np.float32)
    for qb in range(4):
        biasc[0:qb + 1, qb * 256:(qb + 1) * 256] = 0.0
    return dict(cosA=cosA, ssA=ssA, cosB=cosB, ssB=ssB, ident=ident, tri=tri, onehot=onehot, biasc=biasc)


def make_in_maps(inputs, nseq=2, ncores=NCORES):
    f = lambda a: np.ascontiguousarray(np.asarray(a, dtype=np.float32))
    x = f(inputs["x"])
    gains = np.concatenate([f(inputs["mla_q_norm"])[0], f(inputs["mla_kv_norm"])[0]]).reshape(8, 128).T
    shared = dict(
        w_in0=f(inputs["mla_w_in"])[0], gains=np.ascontiguousarray(gains),
        w_uq=f(inputs["mla_w_uq"])[0], w_ukv=f(inputs["mla_w_ukv"])[0], w_o0=f(inputs["mla_w_o"])[0],
        w_kv=f(inputs["moba_w_kv"]), w_in1=f(inputs["moba_w_in"])[0], w_o1=f(inputs["moba_w_o"])[0],
        ln_g=f(inputs["ln_g"]), ln_b=f(inputs["ln_b"]))
    shared.update(host_consts())
    maps = []
    for c in range(ncores):
        m = dict(shared)
        m["x"] = np.ascontiguousarray(x[c * nseq:(c + 1) * nseq])
        maps.append(m)
    return maps


_NC_CACHE = {}


def kernel(**inputs):
    nseq = 2
    if "nc" not in _NC_CACHE:
        _NC_CACHE["nc"] = build_program(nseq=nseq)
    nc = _NC_CACHE["nc"]
    in_maps = make_in_maps(inputs, nseq=nseq)
    res = run_bass_kernel_spmd(nc, in_maps, core_ids=list(range(NCORES)))
    out = np.concatenate([np.asarray(r["out"], dtype=np.float32) for r in res.results], axis=0)
    return out
```

```python
import numpy as np
from contextlib import ExitStack
import concourse.bass as bass
import concourse.mybir as mybir
from concourse.bass_utils import run_bass_kernel_spmd

F32 = mybir.dt.float32
BF16 = mybir.dt.bfloat16
AF = mybir.ActivationFunctionType
ALU = mybir.AluOpType
AX = mybir.AxisListType

NCORES = 8
S = 2048
D = 1024
NT = S // 128
DEPTH = 2
ALPHA = float((2 * DEPTH) ** 0.25)
LN_EPS = 1e-5
RMS_EPS = 1e-6
Q_LORA, KV_LORA, ROPE_A = 768, 256, 32
SCALE_A = float(96 ** -0.5)
SCALE_B = float(64 ** -0.5)
NEG_BIG = -30000.0
THETA = 10000.0


class Sched:
    def __init__(self, nc, stack):
        self.nc = nc
        self.stack = stack
        self.ops = []
        self.last_w = {}
        self.readers = {}
        self.eng_obj = {"pe": nc.tensor, "act": nc.scalar, "dve": nc.vector,
                        "pool": nc.gpsimd, "sp": nc.sync}

    def add(self, eng, fn, reads=(), writes=(), dma=None):
        idx = len(self.ops)
        deps = {}
        for b in reads:
            p = self.last_w.get(b)
            if p is not None:
                deps[p] = True
            if isinstance(b, tuple) and b[0] == "ps":
                for r in self.readers.get(b, ()):
                    if r not in deps:
                        deps[r] = False
        for b in writes:
            p = self.last_w.get(b)
            if p is not None:
                deps[p] = True
            for r in self.readers.get(b, ()):
                if r not in deps:
                    deps[r] = False
        self.ops.append(dict(eng=eng, fn=fn, deps=deps, dma=dma, signal=False, cnt=None))
        for b in writes:
            self.last_w[b] = idx
            self.readers[b] = []
        for b in reads:
            if b not in writes:
                self.readers.setdefault(b, []).append(idx)
        return idx

    def emit(self, final_wait_engine="sp"):
        nc = self.nc
        ops = self.ops
        need = []
        for i, op in enumerate(ops):
            lst = []
            for p, strong in op["deps"].items():
                po = ops[p]
                if po["dma"] is not None:
                    lst.append(p)
                elif po["eng"] == op["eng"] and op["dma"] is None:
                    if op["eng"] == "pe":
                        continue
                    lst.append(p)
                else:
                    lst.append(p)
            youngest = {}
            keep = []
            for p in lst:
                po = ops[p]
                if po["dma"] is not None:
                    keep.append(p)
                else:
                    e_ = po["eng"]
                    if e_ not in youngest or p > youngest[e_]:
                        youngest[e_] = p
            keep += list(youngest.values())
            lst = keep
            need.append(lst)
            for p in lst:
                ops[p]["signal"] = True
        sem = {}
        cnt = {}

        def get_sem(key):
            if key not in sem:
                sem[key] = self.stack.enter_context(nc.semaphore("s%d" % len(sem)))
                cnt[key] = 0
            return sem[key]

        for op in ops:
            if op["dma"] is not None:
                key = ("dma", op["dma"])
                get_sem(key)
                cnt[key] += 16
                op["cnt"] = cnt[key]
                op["semkey"] = key
            elif op["signal"]:
                key = ("eng", op["eng"])
                get_sem(key)
                cnt[key] += 1
                op["cnt"] = cnt[key]
                op["semkey"] = key
        self.n_sems = len(sem)
        self.max_cnt = dict(cnt)
        seen = {e: {} for e in self.eng_obj}
        done_vc = {}
        self.n_waits = 0
        for i, op in enumerate(ops):
            e = op["eng"]
            eo = self.eng_obj[e]
            waits = {}
            for p in need[i]:
                po = ops[p]
                k = po["semkey"]
                waits[k] = max(waits.get(k, 0), po["cnt"])
            for k, v in waits.items():
                if seen[e].get(k, 0) >= v:
                    continue
                eo.wait_ge(sem[k], v)
                self.n_waits += 1
                seen[e][k] = v
                for k2, v2 in done_vc.get((k, v), {}).items():
                    if seen[e].get(k2, 0) < v2:
                        seen[e][k2] = v2
            ins = op["fn"](eo)
            if op["dma"] is not None:
                ins.then_inc(sem[op["semkey"]], 16)
                vc = dict(seen[e])
                vc[op["semkey"]] = op["cnt"]
                done_vc[(op["semkey"], op["cnt"])] = vc
            elif op["signal"]:
                ins.then_inc(sem[op["semkey"]], 1)
                vc = dict(seen[e])
                vc[op["semkey"]] = op["cnt"]
                done_vc[(op["semkey"], op["cnt"])] = vc
        eo = self.eng_obj[final_wait_engine]
        for k, v in cnt.items():
            if k[0] == "dma" and seen[final_wait_engine].get(k, 0) < v:
                eo.wait_ge(sem[k], v)


class Rot:
    def __init__(self, items):
        self.items = list(items)
        self.i = 0

    def next(self):
        v = self.items[self.i % len(self.items)]
        self.i += 1
        return v


def build_program(nseq=2, layers=(0, 1), dbg_x1=False, max_ops=None):
    nc = bass.Bass("TRN2", target_bir_lowering=False)

    def din(name, shape):
        return nc.dram_tensor(name, list(shape), F32, kind="ExternalInput").ap()

    x_d = din("x", [nseq, S, D])
    w_in0_d = din("w_in0", [D, 2080])
    gains_d = din("gains", [128, 8])
    w_uq_d = din("w_uq", [Q_LORA, 1536])
    w_ukv_d = din("w_ukv", [KV_LORA, 2048])
    w_o0_d = din("w_o0", [D, D])
    w_kv_d = din("w_kv", [D, 2048])
    w_in1_d = din("w_in1", [D, 2048])
    w_o1_d = din("w_o1", [D, D])
    ln_g_d = din("ln_g", [2, D])
    ln_b_d = din("ln_b", [2, D])
    ident_d = din("ident", [128, 128])
    tri_d = din("tri", [128, 128])
    cosA_d = din("cosA", [128, NT, 32])
    ssA_d = din("ssA", [128, NT, 32])
    cosB_d = din("cosB", [128, NT, 64])
    ssB_d = din("ssB", [128, NT, 64])
    onehot_d = din("onehot", [8, S])
    biasc_d = din("biasc", [8, 1024])
    out_d = nc.dram_tensor("out", [nseq, S, D], F32, kind="ExternalOutput").ap()
    if dbg_x1:
        x1_d = nc.dram_tensor("x1dbg", [nseq, S, D], F32, kind="ExternalOutput").ap()
    else:
        x1_d = nc.dram_tensor("x1scr", [nseq, S, D], F32).ap()

    with ExitStack() as st:
        def sb(name, shape, dt):
            return st.enter_context(nc.sbuf_tensor("sb_" + name, list(shape), dt))

        B1 = sb("B1", [128, 8, S], BF16)
        B2 = sb("B2", [128, 8, S], BF16)
        B3 = sb("B3", [128, 8, S], BF16)
        QK = sb("QK", [128, 8192], BF16)
        VG = sb("VG", [128, NT, 3, 64], BF16)
        W1 = sb("W1", [128, 13312], BF16)
        gains = sb("gains", [128, 8], F32)
        ident = sb("ident", [128, 128], BF16)
        tri = sb("tri", [128, 128], BF16)
        ones_bf = sb("ones_bf", [128, 8], BF16)
        mhalf = sb("mhalf", [128, 32], F32)
        tabC = sb("tabC", [128, NT * 64], F32)
        tabS = sb("tabS", [128, NT * 64], F32)
        cosA = tabC[:, 0:NT * 32].rearrange("p (t d) -> p t d", t=NT)
        ssA = tabS[:, 0:NT * 32].rearrange("p (t d) -> p t d", t=NT)
        cosB = tabC[:, :].rearrange("p (t d) -> p t d", t=NT)
        ssB = tabS[:, :].rearrange("p (t d) -> p t d", t=NT)
        g_bc = sb("g_bc", [128, D], F32)
        b_bc = sb("b_bc", [128, D], F32)
        xb = [sb("xb%d" % i, [128, D], BF16) for i in range(2)]
        xres = [sb("xres%d" % i, [128, D], F32) for i in range(4)]
        sq = [sb("sq%d" % i, [128, 512], BF16) for i in range(2)]
        PT = [sb("PT%d" % i, [128, 1024], BF16) for i in range(3)]
        q32 = [sb("q32_%d" % i, [128, 256], F32) for i in range(2)]
        kv32 = [sb("kv32_%d" % i, [128, 256], F32) for i in range(2)]
        Qa = [sb("Qa%d" % i, [128, 2, 96], BF16) for i in range(3)]
        Ka = [sb("Ka%d" % i, [128, 2, 96], BF16) for i in range(3)]
        rt1 = [sb("rt1_%d" % i, [128, 2, 64], F32) for i in range(2)]
        rt2 = [sb("rt2_%d" % i, [128, 2, 64], F32) for i in range(2)]
        krope = sb("krope", [128, NT, 32], BF16)
        wkr = sb("wkr", [128, 8, 32], BF16)
        rstd = sb("rstd", [128, 32], F32)
        rg = [sb("rg%d" % i, [128, 512], F32) for i in range(2)]
        kr1 = rg[0][:, :].rearrange("p (t d) -> p t d", t=NT)
        kr2 = rg[1][:, :].rearrange("p (t d) -> p t d", t=NT)
        gsv = [sb("gsv%d" % i, [128, 2, 8], F32) for i in range(2)]
        cmpt = [sb("cmp%d" % i, [128, 2, 8, 8], F32) for i in range(2)]
        rank = [sb("rank%d" % i, [128, 2, 8], F32) for i in range(2)]
        bias = [sb("bias%d" % i, [128, 2, 8], BF16) for i in range(4)]
        km = sb("km", [64, 2, 8], F32)
        kmT = sb("kmT", [64, 2, 8], BF16)
        lnst = [sb("lnst%d" % i, [128, 2, 6], F32) for i in range(4)]
        lnmv = [sb("lnmv%d" % i, [128, 2], F32) for i in range(4)]
        lnr = [sb("lnr%d" % i, [128, 2], F32) for i in range(4)]
        PS = st.enter_context(nc.psum_tensor("PS", [128, 8, 512], F32))
        ps = [PS[:, i, :] for i in range(8)]

        def psb(i):
            return ps[i].bitcast(BF16)

        qT = QK[0:96, 0:4096].rearrange("p (h s) -> p h s", h=2)
        kT = QK[0:96, 4096:8192].rearrange("p (h s) -> p h s", h=2)
        w2 = [QK[:, i * 4096:(i + 1) * 4096].rearrange("p (k n) -> p k n", k=8) for i in range(2)]
        W1a = W1[:, 0:9216]
        W1b = W1[:, 9216:13312]
        w_uq_sb = W1a.rearrange("p (f n) -> p f n", f=6)
        w_ukv_sb = W1b.rearrange("p (f n) -> p f n", f=2)
        w_o_sb = W1[:, 0:8192].rearrange("p (k n) -> p k n", k=8)
        wg = [W1[:, 0:3072].rearrange("p (k n) -> p k n", k=8),
              W1[:, 9216:12288].rearrange("p (k n) -> p k n", k=8)]

        SC = Sched(nc, st)
        MARKS = {}

        def mark(name):
            MARKS[name] = len(SC.ops)

        def op(eng, meth, *args, reads=(), writes=(), dma=None, **kw):
            SC.add(eng, (lambda e: getattr(e, meth)(*args, **kw)), reads=reads, writes=writes, dma=dma)

        def kB(name, cs, ts):
            return [(name, c, t) for c in cs for t in ts]

        def kB1(cs, ts):
            return [("B1", c, t, hl) for c in cs for t in ts for hl in range(2)]
        R8 = range(8)
        RT = range(NT)
        K_QT = [("qT", h, t) for h in range(2) for t in RT]
        K_KT = [("kT", h, t) for h in range(2) for t in RT]
        K_W2 = [K_QT, K_KT]
        K_W1A = [("w1", 0)]
        K_W1B = [("w1", 1)]

        def kps(i):
            return [("ps", i)]

        def wT(w_d):
            return w_d.rearrange("(kc p) n -> p kc n", p=128)

        def tslice(t):
            return slice(t * 128, (t + 1) * 128)

        op("pool", "dma_start", out=ident[:], in_=ident_d, writes=["ident"], dma="ident")
        op("pool", "dma_start", out=tri[:], in_=tri_d, writes=["tri"], dma="tri")
        op("sp", "dma_start", out=gains[:], in_=gains_d, writes=["gains"], dma="gains")
        op("dve", "memset", ones_bf[:], 1.0, writes=["ones_bf"])
        op("dve", "memset", mhalf[:], -0.5, writes=["mhalf"])
        op("dve", "memset", VG[:, :, 1, :], 1.0, writes=["VG1"])

        gen = Rot([6, 7])

        def pipeline(stages, n):
            for s_ in range(n + len(stages) - 1):
                for k_ in range(len(stages) - 1, -1, -1):
                    t_ = s_ - k_
                    if 0 <= t_ < n:
                        stages[k_](t_)
        pt_rot = Rot([0, 1, 2])
        rg_rot = Rot([0, 1])

        def preload_c(layer, w_o_d):
            op("pool", "dma_start", out=w_o_sb, in_=wT(w_o_d), writes=K_W1A, dma="w1a")
            load_ln(layer)

        def load_ln(layer):
            op("sp", "dma_start", out=g_bc[:], in_=ln_g_d[layer:layer + 1, :].to_broadcast([128, D]),
               writes=["g_bc"], dma="g_bc")
            op("sp", "dma_start", out=b_bc[:], in_=ln_b_d[layer:layer + 1, :].to_broadcast([128, D]),
               writes=["b_bc"], dma="b_bc")

        def transposes_to(src_bf, srckey, DST, dkeys, t, evac_eng, bank=None):
            b = gen.next() if bank is None else bank
            pv = psb(b)
            for kc in range(8):
                op("pe", "transpose", pv[:, kc * 128:(kc + 1) * 128], src_bf[:, kc * 128:(kc + 1) * 128], ident[:],
                   reads=[srckey, "ident"], writes=kps(b))
            src = pv[:, 0:1024].rearrange("p (c n) -> p c n", c=8)
            if evac_eng == "dve":
                op("dve", "tensor_copy", DST[:, :, tslice(t)], src, reads=kps(b), writes=dkeys)
            else:
                op("act", "copy", DST[:, :, tslice(t)], src, reads=kps(b), writes=dkeys)

        def attention_group(g, Kd, scale, L=2):
            slot = Rot([0, 1, 3])
            otb = Rot([4, 5])
            units = []
            for hl in range(2):
                for qc in range(4):
                    ob = otb.next()
                    nj = 4 * qc + 4
                    for j in range(0, 4 * qc, 2):
                        units.append(dict(hl=hl, qc=qc, js=[j, j + 1], nj=nj, ob=ob, diag=False))
                    for j in range(4 * qc, nj):
                        units.append(dict(hl=hl, qc=qc, js=[j], nj=nj, ob=ob, diag=True))

            def front(u):
                hl, qc = u["hl"], u["qc"]
                sl = slot.next()
                pi = pt_rot.next()
                if not u["diag"]:
                    qlo, n = qc * 512, 512
                    tq = list(range(qc * 4, qc * 4 + 4))
                    for k, j in enumerate(u["js"]):
                        bank = 2 * sl + k
                        op("pe", "matmul", ps[bank][:, 0:512], kT[0:Kd, hl, j * 128:(j + 1) * 128],
                           qT[0:Kd, hl, qlo:qlo + 512], start=True, stop=True,
                           reads=[("kT", hl, j)] + [("qT", hl, t) for t in tq], writes=kps(bank))
                    op("act", "activation", PT[pi][:, :].rearrange("p (b n) -> p b n", b=2),
                       PS[:, 2 * sl:2 * sl + 2, :], AF.Exp, scale=scale,
                       reads=kps(2 * sl) + kps(2 * sl + 1), writes=[("PT", pi)])
                else:
                    j = u["js"][0]
                    qlo = j * 128
                    n = (qc + 1) * 512 - qlo
                    bank = 2 * sl
                    tq = list(range(j, (qc + 1) * 4))
                    op("pe", "matmul", ps[bank][:, 0:n], kT[0:Kd, hl, j * 128:(j + 1) * 128],
                       qT[0:Kd, hl, qlo:qlo + n], start=True, stop=True,
                       reads=[("kT", hl, j)] + [("qT", hl, t) for t in tq], writes=kps(bank))
                    op("act", "activation", PT[pi][:, 0:n], ps[bank][:, 0:n], AF.Exp, scale=scale,
                       reads=kps(bank), writes=[("PT", pi)])
                    op("pool", "tensor_tensor", PT[pi][:, 0:128], PT[pi][:, 0:128], tri[:], ALU.mult,
                       reads=[("PT", pi), "tri"], writes=[("PT", pi)])
                u.update(pi=pi, qlo=qlo, n=n)

            def back(u):
                hl, qc, nj, ob, pi, n, qlo = (u[k] for k in ("hl", "qc", "nj", "ob", "pi", "n", "qlo"))
                o0 = qlo - qc * 512
                for k, j in enumerate(u["js"]):
                    vT = VG[:, j, hl:hl + 2, :].rearrange("p a d -> p (a d)")
                    op("pe", "matmul", ps[ob][:, o0:512], vT, PT[pi][:, k * 512:k * 512 + n],
                       start=(j == 0), stop=(j == nj - 1),
                       reads=[("PT", pi), ("VG", hl, j), "VG1"], writes=kps(ob))
                if u["js"][-1] == nj - 1:
                    p0 = hl * 64
                    p1 = 64 - p0
                    ri = rg_rot.next()
                    cols = slice(qc * 512, (qc + 1) * 512)
                    ts4 = list(range(qc * 4, qc * 4 + 4))
                    op("dve", "reciprocal", rg[ri][p0:p0 + 64, :], ps[ob][p1:p1 + 64, :],
                       reads=kps(ob), writes=[("rg", ri)])
                    op("dve", "tensor_tensor", rg[ri][p0:p0 + 64, :], rg[ri][p0:p0 + 64, :],
                       B2[p0:p0 + 64, g, cols], ALU.mult,
                       reads=[("rg", ri)] + kB("B2", [g], ts4), writes=[("rg", ri)])
                    op("dve", "tensor_tensor", B1[p0:p0 + 64, g, cols], ps[ob][p0:p0 + 64, :],
                       rg[ri][p0:p0 + 64, :], ALU.mult,
                       reads=kps(ob) + [("rg", ri)], writes=[("B1", g, t, hl) for t in ts4])

            for i in range(len(units) + L):
                if i < len(units):
                    front(units[i])
                if i >= L:
                    back(units[i - L])

        def phase_c(s, layer, src_d, dst_d, to_B3, w_o_d, pre_fn=None, tail_stage=None):
            if pre_fn is not None:
                pre_fn()
            ybanks = Rot([0, 1, 2, 3, 4, 5])
            ybk = {}

            def c1(t):
                i = t % 4
                tok = tslice(t)
                xr = xres[i]
                kx = [("xres", i)]
                op("sp", "dma_start", out=xr[:], in_=src_d[s, tok, :],
                   reads=[("xsrc", layer, s, t)], writes=kx, dma=("xres", i))
                yb = [ybanks.next(), ybanks.next()]
                ybk[t] = yb
                for half in range(2):
                    for c in range(8):
                        op("pe", "matmul", ps[yb[half]][:, :], B1[:, c, tok],
                           w_o_sb[:, c, half * 512:(half + 1) * 512], start=(c == 0), stop=(c == 7),
                           reads=kB1([c], [t]) + K_W1A, writes=kps(yb[half]))

            def c2(t):
                i = t % 4
                xr = xres[i]
                kx = [("xres", i)]
                yb = ybk[t]
                for half in range(2):
                    hs = slice(half * 512, (half + 1) * 512)
                    op("dve", "scalar_tensor_tensor", xr[:, hs], xr[:, hs], ALPHA, ps[yb[half]][:, :],
                       ALU.mult, ALU.add, reads=kx + kps(yb[half]), writes=kx)
                    op("dve", "bn_stats", lnst[i][:, half, :], xr[:, hs], reads=kx, writes=[("lnst", i)])
                op("dve", "bn_aggr", lnmv[i][:], lnst[i][:].rearrange("p a b -> p (a b)"),
                   reads=[("lnst", i)], writes=[("lnmv", i)])
                op("dve", "tensor_scalar", lnr[i][:, 0:1], lnmv[i][:, 1:2], LN_EPS, None, ALU.add,
                   reads=[("lnmv", i)], writes=[("lnr", i)])
                op("pool", "tensor_tensor", lnr[i][:, 0:1], lnr[i][:, 0:1], mhalf[:, 0:1], ALU.pow,
                   reads=[("lnr", i), "mhalf"], writes=[("lnr", i)])

            def c3(t):
                i = t % 4
                tok = tslice(t)
                xr = xres[i]
                kx = [("xres", i)]
                op("dve", "scalar_tensor_tensor", xr[:], xr[:], lnmv[i][:, 0:1], g_bc[:], ALU.subtract, ALU.mult,
                   reads=kx + [("lnmv", i), "g_bc"], writes=kx)
                op("act", "activation", xr[:], xr[:], AF.Identity, scale=lnr[i][:, 0:1],
                   reads=kx + [("lnr", i)], writes=kx)

            def c3b(t):
                i = t % 4
                tok = tslice(t)
                xr = xres[i]
                kx = [("xres", i)]
                op("pool", "tensor_tensor", xr[:], xr[:], b_bc[:], ALU.add, reads=kx + ["b_bc"], writes=kx)
                op("sp", "dma_start", out=dst_d[s, tok, :], in_=xr[:],
                   reads=kx, writes=[("xsrc", layer + 1, s, t)], dma=("xres", i))
                if to_B3:
                    j = t % 2
                    op("act", "copy", xb[j][:], xr[:], reads=kx, writes=[("xb", j)])

            c4b = {}

            def c4(t):
                j = t % 2
                b = gen.next()
                c4b[t] = b
                pv = psb(b)
                for kc in range(8):
                    op("pe", "transpose", pv[:, kc * 128:(kc + 1) * 128], xb[j][:, kc * 128:(kc + 1) * 128], ident[:],
                       reads=[("xb", j), "ident"], writes=kps(b))

            def c5(t):
                b = c4b[t]
                pv = psb(b)
                op("act", "copy", B3[:, :, tslice(t)], pv[:, 0:1024].rearrange("p (c n) -> p c n", c=8),
                   reads=kps(b), writes=kB("B3", R8, [t]))

            stages = [c1, c2, c3, c3b] + ([c4, c5] if to_B3 else []) + ([tail_stage] if tail_stage is not None else [])
            pipeline(stages, NT)

        def rope_ops(i, src3, cos_t, ss_t, hd, dst3, dstkey, srckeys, ck, sk):
            hh = hd // 2
            op("dve", "tensor_tensor", rt1[i][:, :, 0:hd], src3, cos_t.to_broadcast([128, 2, hd]), ALU.mult,
               reads=srckeys + [ck], writes=[("rt1", i)])
            op("dve", "tensor_tensor", rt2[i][:, :, 0:hh], src3[:, :, hh:hd],
               ss_t[:, :, 0:hh].to_broadcast([128, 2, hh]), ALU.mult,
               reads=srckeys + [sk], writes=[("rt2", i)])
            op("dve", "tensor_tensor", rt2[i][:, :, hh:hd], src3[:, :, 0:hh],
               ss_t[:, :, hh:hd].to_broadcast([128, 2, hh]), ALU.mult,
               reads=srckeys + [sk], writes=[("rt2", i)])
            op("dve", "tensor_tensor", dst3, rt1[i][:, :, 0:hd], rt2[i][:, :, 0:hd], ALU.add,
               reads=[("rt1", i), ("rt2", i)], writes=[dstkey])

        l0_done = set()

        def l0_prefetch(s):
            op("pool", "dma_start", out=w2[0], in_=wT(w_in0_d)[:, :, 0:512], writes=K_W2[0], dma=("w2", 0))
            op("pool", "dma_start", out=wkr[:], in_=wT(w_in0_d)[:, :, 1024:1056], writes=["wkr"], dma="wkr")

        def l0_a0(s, t, bank=6):
            i = t % 2
            op("pool", "dma_start", out=xb[i][:], in_=x_d[s, tslice(t), :], writes=[("xb", i)], dma=("xb", i))
            transposes_to(xb[i], ("xb", i), B1, kB1(R8, [t]), t, "dve" if t % 2 == 0 else "act", bank=bank)

        def l1_prefetch():
            for sl in range(2):
                op("pool", "dma_start", out=w2[sl], in_=wT(w_in1_d)[:, :, 1024 + sl * 512:1024 + (sl + 1) * 512],
                   writes=K_W2[sl], dma=("w2", sl))

        def layer0(s):
            op("sp", "dma_start", out=cosA, in_=cosA_d, writes=["tabC"], dma="tabC")
            op("sp", "dma_start", out=ssA, in_=ssA_d, writes=["tabS"], dma="tabS")
            col0 = [0, 512, 1056, 1568]
            prefetched = s in l0_done
            if not prefetched:
                l0_prefetch(s)

            def a0(t):
                if not prefetched:
                    l0_a0(s, t)

            for t in range(4):
                a0(t)
            op("pool", "dma_start", out=w_uq_sb, in_=w_uq_d.rearrange("(f p) n -> p f n", p=128),
               writes=K_W1A, dma="w1a")
            op("pool", "dma_start", out=w_ukv_sb, in_=w_ukv_d.rearrange("(f p) n -> p f n", p=128),
               writes=K_W1B, dma="w1b")
            mark('A0_end')
            op("dve", "memset", ps[7][:, 0:32], 0.0, writes=kps(7))
            banks = Rot([0, 1, 2, 3, 4, 5])
            sqr = Rot([0, 1])
            pend = []

            def flush_stats(keep):
                while len(pend) > keep:
                    si_, c4_, f_ = pend.pop(0)
                    for tt in range(4):
                        col = (c4_ * 4 + tt) * 2 + (0 if f_ < 6 else 1)
                        op("pe", "matmul", ps[7][:, col:col + 1], sq[si_][:, tt * 128:(tt + 1) * 128],
                           ones_bf[:, 0:1], start=False, stop=False, skip_group_check=True,
                           reads=[("sq", si_), "ones_bf"], writes=kps(7))
            for sl in range(4):
                wi = sl % 2
                if sl > 0:
                    op("pool", "dma_start", out=w2[wi], in_=wT(w_in0_d)[:, :, col0[sl]:col0[sl] + 512],
                       writes=K_W2[wi], dma=("w2", wi))
                for c4 in range(4):
                    cols = slice(c4 * 512, (c4 + 1) * 512)
                    ts4 = list(range(c4 * 4, c4 * 4 + 4))
                    if sl == 0 and c4 < 3:
                        for t_ in range(4 * c4 + 4, 4 * c4 + 8):
                            a0(t_)
                    for j in range(4):
                        b = banks.next()
                        for kc in range(8):
                            op("pe", "matmul", ps[b][:, :], w2[wi][:, kc, j * 128:(j + 1) * 128], B1[:, kc, cols],
                               start=(kc == 0), stop=(kc == 7),
                               reads=K_W2[wi] + kB1([kc], ts4), writes=kps(b))
                        flush_stats(1)
                        if sl < 2:
                            f = sl * 4 + j
                            op("dve", "tensor_scalar", B3[:, f, cols], ps[b][:, :], gains[:, f:f + 1], None, ALU.mult,
                               reads=kps(b) + ["gains"], writes=kB("B3", [f], ts4))
                            si = sqr.next()
                            op("act", "activation", sq[si][:], ps[b][:, :], AF.Square,
                               reads=kps(b), writes=[("sq", si)])
                            pend.append((si, c4, f))
                        else:
                            f = (sl - 2) * 4 + j
                            op("act", "activation", B2[:, f, cols], ps[b][:, :], AF.Silu,
                               reads=kps(b), writes=kB("B2", [f], ts4))
            flush_stats(0)
            mark('A1_slabs_end')
            op("dve", "tensor_scalar", rstd[:, 0:32:2], ps[7][:, 0:32:2], 1.0 / Q_LORA, RMS_EPS, ALU.mult, ALU.add,
               reads=kps(7), writes=["rstd"])
            op("dve", "tensor_scalar", rstd[:, 1:32:2], ps[7][:, 1:32:2], 1.0 / KV_LORA, RMS_EPS, ALU.mult, ALU.add,
               reads=kps(7), writes=["rstd"])
            op("pool", "tensor_tensor", rstd[:], rstd[:], mhalf[:], ALU.pow, reads=["rstd", "mhalf"], writes=["rstd"])
            for t in RT:
                for kc in range(8):
                    op("pe", "matmul", ps[6][:, t * 32:(t + 1) * 32], B1[:, kc, tslice(t)], wkr[:, kc, :],
                       start=(kc == 0), stop=(kc == 7), reads=kB1([kc], [t]) + ["wkr"], writes=kps(6))
            p6 = ps[6][:, :].rearrange("p (t d) -> p t d", t=NT)
            op("dve", "tensor_tensor", kr1, p6, cosA, ALU.mult, reads=kps(6) + ["tabC"], writes=[("rg", 0)])
            op("dve", "tensor_tensor", kr2[:, :, 0:16], p6[:, :, 16:32], ssA[:, :, 0:16], ALU.mult,
               reads=kps(6) + ["tabS"], writes=[("rg", 1)])
            op("dve", "tensor_tensor", kr2[:, :, 16:32], p6[:, :, 0:16], ssA[:, :, 16:32], ALU.mult,
               reads=kps(6) + ["tabS"], writes=[("rg", 1)])
            op("dve", "tensor_tensor", krope[:], kr1, kr2, ALU.add, reads=[("rg", 0), ("rg", 1)], writes=["krope"])
            mark('A1_end')
            pa_rot = Rot([0, 1, 2, 3])
            pc_rot = Rot([6, 7])
            for g in range(8):
                st_ = {}
                sc_ = {}

                def s0(t, g=g, st_=st_):
                    tok = tslice(t)
                    b = pa_rot.next()
                    st_[t] = b
                    for f in range(6):
                        op("pe", "matmul", ps[b][:, 0:192], B3[:, f, tok], w_uq_sb[:, f, g * 192:(g + 1) * 192],
                           start=(f == 0), stop=(f == 5), reads=kB("B3", [f], [t]) + K_W1A, writes=kps(b))
                    for f in range(2):
                        op("pe", "matmul", ps[b][:, 256:512], B3[:, 6 + f, tok], w_ukv_sb[:, f, g * 256:(g + 1) * 256],
                           start=(f == 0), stop=(f == 1), reads=kB("B3", [6 + f], [t]) + K_W1B, writes=kps(b))

                def s1(t, st_=st_):
                    i = t % 2
                    b = st_[t]
                    op("act", "activation", q32[i][:, 0:192], ps[b][:, 0:192], AF.Identity,
                       scale=rstd[:, 2 * t:2 * t + 1], reads=kps(b) + ["rstd"], writes=[("q32", i)])
                    op("act", "activation", kv32[i][:, :], ps[b][:, 256:512], AF.Identity,
                       scale=rstd[:, 2 * t + 1:2 * t + 2], reads=kps(b) + ["rstd"], writes=[("kv32", i)])

                def s2(t):
                    i = t % 2
                    j = t % 3
                    hd, hh = 32, 16
                    qv = q32[i][:, 0:192].rearrange("p (h d) -> p h d", h=2)
                    kvv = kv32[i][:, :].rearrange("p (h d) -> p h d", h=2)
                    src3 = qv[:, :, 64:96]
                    op("dve", "tensor_tensor", rt1[i][:, :, 0:hd], src3, cosA[:, t:t + 1, :].to_broadcast([128, 2, hd]), ALU.mult,
                       reads=[("q32", i), "tabC"], writes=[("rt1", i)])
                    op("dve", "tensor_tensor", rt2[i][:, :, 0:hh], src3[:, :, hh:hd],
                       ssA[:, t:t + 1, 0:hh].to_broadcast([128, 2, hh]), ALU.mult,
                       reads=[("q32", i), "tabS"], writes=[("rt2", i)])
                    op("dve", "tensor_tensor", rt2[i][:, :, hh:hd], src3[:, :, 0:hh],
                       ssA[:, t:t + 1, hh:hd].to_broadcast([128, 2, hh]), ALU.mult,
                       reads=[("q32", i), "tabS"], writes=[("rt2", i)])
                    op("pool", "tensor_copy", Qa[j][:, :, 0:64], qv[:, :, 0:64], reads=[("q32", i)], writes=[("Qa", j)])
                    op("pool", "tensor_copy", Ka[j][:, :, 0:64], kvv[:, :, 0:64], reads=[("kv32", i)], writes=[("Ka", j)])
                    op("pool", "tensor_copy", Ka[j][:, :, 64:96], krope[:, t:t + 1, :].to_broadcast([128, 2, 32]),
                       reads=["krope"], writes=[("Ka", j)])
                    op("pool", "tensor_copy", VG[:, t, 0:3:2, :], kvv[:, :, 64:128],
                       reads=[("kv32", i)], writes=[("VG", 0, t), ("VG", 1, t)])

                def s3(t):
                    i = t % 2
                    j = t % 3
                    op("dve", "tensor_tensor", Qa[j][:, :, 64:96], rt1[i][:, :, 0:32], rt2[i][:, :, 0:32], ALU.add,
                       reads=[("rt1", i), ("rt2", i)], writes=[("Qa", j)])

                def s4(t, sc_=sc_):
                    j = t % 3
                    b2 = pc_rot.next()
                    sc_[t] = b2
                    pv = psb(b2)
                    for hl in range(2):
                        op("pe", "transpose", pv[0:96, hl * 128:(hl + 1) * 128], Qa[j][:, hl, :], ident[:],
                           reads=[("Qa", j), "ident"], writes=kps(b2))
                    for hl in range(2):
                        op("pe", "transpose", pv[0:96, 256 + hl * 128:256 + (hl + 1) * 128], Ka[j][:, hl, :], ident[:],
                           reads=[("Ka", j), "ident"], writes=kps(b2))

                def s5(t, sc_=sc_):
                    tok = tslice(t)
                    b2 = sc_[t]
                    pv = psb(b2)
                    dst = QK[0:96, :].rearrange("p (a s) -> p a s", a=4)[:, :, tok]
                    srcv = pv[0:96, 0:512].rearrange("p (a n) -> p a n", a=4)
                    wk = [("qT", 0, t), ("qT", 1, t), ("kT", 0, t), ("kT", 1, t)]
                    if t % 2 == 0:
                        op("act", "copy", dst, srcv, reads=kps(b2), writes=wk)
                    else:
                        op("dve", "tensor_copy", dst, srcv, reads=kps(b2), writes=wk)

                pipeline([s0, s1, s2, s3, s4, s5], NT)
                mark('L0_g%d_proj_end' % g)
                if g == 7:
                    preload_c(0, w_o0_d)
                attention_group(g, 96, SCALE_A)
                mark('L0_g%d_att_end' % g)
            phase_c(s, 0, x_d, x1_d, True, w_o0_d, pre_fn=l1_prefetch)
            mark('L0_end')

        def layer1(s):
            op("sp", "dma_start", out=cosB, in_=cosB_d, writes=["tabC"], dma="tabC")
            op("sp", "dma_start", out=ssB, in_=ssB_d, writes=["tabS"], dma="tabS")
            def load_wg(g):
                gi = g % 2
                key = K_W1A if gi == 0 else K_W1B
                op("pool", "dma_start", out=wg[gi][:, :, 0:128], in_=wT(w_in1_d)[:, :, g * 128:(g + 1) * 128],
                   writes=key, dma=("wg", gi))
                op("pool", "dma_start", out=wg[gi][:, :, 128:256], in_=wT(w_kv_d)[:, :, g * 128:(g + 1) * 128],
                   writes=key, dma=("wg", gi))
                op("pool", "dma_start", out=wg[gi][:, :, 256:384],
                   in_=wT(w_kv_d)[:, :, 1024 + g * 128:1024 + (g + 1) * 128], writes=key, dma=("wg", gi))

            load_wg(0)
            banks = Rot([0, 1, 2, 3, 4, 5])
            for sl in range(2):
                wi = sl % 2
                for c4 in range(4):
                    cols = slice(c4 * 512, (c4 + 1) * 512)
                    ts4 = list(range(c4 * 4, c4 * 4 + 4))
                    for j in range(4):
                        b = banks.next()
                        f = sl * 4 + j
                        for kc in range(8):
                            op("pe", "matmul", ps[b][:, :], w2[wi][:, kc, j * 128:(j + 1) * 128], B3[:, kc, cols],
                               start=(kc == 0), stop=(kc == 7),
                               reads=K_W2[wi] + kB("B3", [kc], ts4), writes=kps(b))
                        op("act", "activation", B2[:, f, cols], ps[b][:, :], AF.Silu,
                           reads=kps(b), writes=kB("B2", [f], ts4))
            for hl in range(2):
                op("pool", "dma_start", out=kT[64:72, hl, :], in_=onehot_d,
                   writes=[("kT", hl, t) for t in RT], dma=("oh", hl))
                op("pool", "dma_start", out=qT[64:72, hl, 0:1024], in_=biasc_d,
                   writes=[("qT", hl, t) for t in range(8)], dma=("oh", hl))

            for g in range(8):
                gi = g % 2
                wkey = K_W1A if gi == 0 else K_W1B
                if g + 1 < 8:
                    load_wg(g + 1)
                if g == 7:
                    preload_c(1, w_o1_d)
                pa_rot = Rot([0, 1, 2, 3])
                pc_rot = Rot([4, 5])
                pd_rot = Rot([6, 7])
                stk = {}
                stc = {}

                def p0_(t, gi=gi, wkey=wkey, stk=stk):
                    tok = tslice(t)
                    b = pa_rot.next()
                    stk[t] = b
                    for kc in range(8):
                        op("pe", "matmul", ps[b][:, 0:384], B3[:, kc, tok], wg[gi][:, kc, 0:384],
                           start=(kc == 0), stop=(kc == 7), reads=kB("B3", [kc], [t]) + wkey, writes=kps(b))

                def p1_(t, stk=stk):
                    i = t % 2
                    b = stk[t]
                    src4 = ps[b][:, 0:256].rearrange("p (h d) -> p h d", h=4)
                    r1 = q32[i][:, :].rearrange("p (h d) -> p h d", h=4)
                    r2 = kv32[i][:, :].rearrange("p (h d) -> p h d", h=4)
                    op("dve", "tensor_tensor", r1, src4, cosB[:, t:t + 1, :].to_broadcast([128, 4, 64]), ALU.mult,
                       reads=kps(b) + ["tabC"], writes=[("q32", i)])
                    op("dve", "tensor_tensor", r2[:, :, 0:32], src4[:, :, 32:64],
                       ssB[:, t:t + 1, 0:32].to_broadcast([128, 4, 32]), ALU.mult,
                       reads=kps(b) + ["tabS"], writes=[("kv32", i)])
                    op("dve", "tensor_tensor", r2[:, :, 32:64], src4[:, :, 0:32],
                       ssB[:, t:t + 1, 32:64].to_broadcast([128, 4, 32]), ALU.mult,
                       reads=kps(b) + ["tabS"], writes=[("kv32", i)])
                    op("act", "copy", VG[:, t, 0:3:2, :], ps[b][:, 256:384].rearrange("p (h d) -> p h d", h=2),
                       reads=kps(b), writes=[("VG", 0, t), ("VG", 1, t)])

                def p2_(t):
                    i = t % 2
                    j = t % 3
                    r1 = q32[i][:, :].rearrange("p (h d) -> p h d", h=4)
                    r2 = kv32[i][:, :].rearrange("p (h d) -> p h d", h=4)
                    op("pool", "tensor_tensor", Qa[j][:, :, 0:64], r1[:, 0:2, :], r2[:, 0:2, :], ALU.add,
                       reads=[("q32", i), ("kv32", i)], writes=[("Qa", j)])
                    op("pool", "tensor_tensor", Ka[j][:, :, 0:64], r1[:, 2:4, :], r2[:, 2:4, :], ALU.add,
                       reads=[("q32", i), ("kv32", i)], writes=[("Ka", j)])

                def p3_(t, stc=stc):
                    j = t % 3
                    b2 = pc_rot.next()
                    stc[t] = b2
                    pv = psb(b2)
                    for hl in range(2):
                        op("pe", "transpose", pv[0:64, hl * 128:(hl + 1) * 128], Qa[j][:, hl, 0:64], ident[:],
                           reads=[("Qa", j), "ident"], writes=kps(b2))
                    for hl in range(2):
                        op("pe", "transpose", pv[0:64, 256 + hl * 128:256 + (hl + 1) * 128], Ka[j][:, hl, 0:64], ident[:],
                           reads=[("Ka", j), "ident"], writes=kps(b2))

                def p4_(t, stc=stc):
                    tok = tslice(t)
                    b2 = stc[t]
                    pv = psb(b2)
                    dst = QK[0:64, :].rearrange("p (a s) -> p a s", a=4)[:, :, tok]
                    srcv = pv[0:64, 0:512].rearrange("p (a n) -> p a n", a=4)
                    wk = [("qT", 0, t), ("qT", 1, t), ("kT", 0, t), ("kT", 1, t)]
                    if t % 2 == 0:
                        op("act", "copy", dst, srcv, reads=kps(b2), writes=wk)
                    else:
                        op("dve", "tensor_copy", dst, srcv, reads=kps(b2), writes=wk)

                pipeline([p0_, p1_, p2_, p3_, p4_], NT)
                for hl in range(2):
                    op("dve", "tensor_reduce", km[:, hl, :], kT[0:64, hl, :].rearrange("p (n k) -> p n k", n=8),
                       AX.X, ALU.add, reads=[("kT", hl, t) for t in RT], writes=["km"])
                op("dve", "tensor_scalar", kmT[:], km[:], 1.0 / 256.0, None, ALU.mult, reads=["km"], writes=["kmT"])
                sg = {}
                sd = {}

                def t0_(t0, sg=sg):
                    t = 8 + t0
                    bi = t % 4
                    qb = t // 2
                    tok = tslice(t)
                    b3 = pc_rot.next()
                    sg[t] = b3
                    for hl in range(2):
                        op("pe", "matmul", ps[b3][:, hl * 8:(hl + 1) * 8], qT[0:64, hl, tok], kmT[:, hl, :],
                           start=True, stop=True, reads=[("qT", hl, t), "kmT"], writes=kps(b3))
                    op("pool", "memset", bias[bi][:], NEG_BIG, writes=[("bias", bi)])
                    op("pool", "memset", bias[bi][:, :, qb:qb + 1], 0.0, writes=[("bias", bi)])

                def t1_(t0, sg=sg):
                    t = 8 + t0
                    i = t % 2
                    bi = t % 4
                    qb = t // 2
                    b3 = sg[t]
                    op("dve", "tensor_copy", gsv[i][:], ps[b3][:, 0:16].rearrange("p (h n) -> p h n", h=2),
                       reads=kps(b3), writes=[("gsv", i)])
                    op("dve", "tensor_tensor", cmpt[i][:, :, 0:qb, 0:qb],
                       gsv[i][:, :, 0:qb].unsqueeze(2).to_broadcast([128, 2, qb, qb]),
                       gsv[i][:, :, 0:qb].unsqueeze(3).to_broadcast([128, 2, qb, qb]), ALU.is_gt,
                       reads=[("gsv", i)], writes=[("cmp", i)])
                    op("dve", "tensor_reduce", rank[i][:, :, 0:qb], cmpt[i][:, :, 0:qb, 0:qb], AX.X, ALU.add,
                       reads=[("cmp", i)], writes=[("rank", i)])
                    op("dve", "tensor_scalar", bias[bi][:, :, 0:qb], rank[i][:, :, 0:qb], 3.0, NEG_BIG,
                       ALU.is_ge, ALU.mult, reads=[("rank", i)], writes=[("bias", bi)])

                def t2_(t0, sd=sd):
                    t = 8 + t0
                    bi = t % 4
                    b4 = pd_rot.next()
                    sd[t] = b4
                    pv4 = psb(b4)
                    for hl in range(2):
                        op("pe", "transpose", pv4[0:8, hl * 128:(hl + 1) * 128], bias[bi][:, hl, :], ident[:],
                           reads=[("bias", bi), "ident"], writes=kps(b4))

                def t3_(t0, sd=sd):
                    t = 8 + t0
                    tok = tslice(t)
                    pv4 = psb(sd[t])
                    op("act", "copy", qT[64:72, :, tok], pv4[0:8, 0:256].rearrange("p (h n) -> p h n", h=2),
                       reads=kps(sd[t]), writes=[("qT", 0, t), ("qT", 1, t)])

                pipeline([t0_, t1_, t2_, t3_], 8)
                attention_group(g, 72, SCALE_B)
            if s + 1 < nseq:
                l0_done.add(s + 1)
                phase_c(s, 1, x1_d, out_d, False, w_o1_d, pre_fn=(lambda: l0_prefetch(s + 1)),
                        tail_stage=(lambda t: l0_a0(s + 1, t, bank=6 + (t % 2))))
            else:
                phase_c(s, 1, x1_d, out_d, False, w_o1_d)

        for s in range(nseq):
            if 0 in layers:
                layer0(s)
            if 1 in layers:
                layer1(s)
        build_program.marks = dict(MARKS)
        if max_ops is not None:
            SC.ops = SC.ops[:max_ops]
        SC.emit()
        build_program.info = dict(n_ops=len(SC.ops), n_sems=SC.n_sems, max_cnt=SC.max_cnt, n_waits=SC.n_waits)
    return nc


def host_consts():
    pos = np.arange(S, dtype=np.float32)

    def tables(dim):
        inv = (np.float32(THETA) ** (-np.arange(0, dim, 2, dtype=np.float32) / np.float32(dim))).astype(np.float32)
        ang = (pos[:, None] * inv[None, :]).astype(np.float32)
        ang = np.concatenate([ang, ang], axis=-1)
        cos = np.cos(ang).astype(np.float32)
        sin = np.sin(ang).astype(np.float32)
        ss = sin.copy()
        ss[:, :dim // 2] = -ss[:, :dim // 2]
        tok = lambda a: np.ascontiguousarray(a.reshape(NT, 128, dim).transpose(1, 0, 2))
        return tok(cos), tok(ss)

    cosA, ssA = tables(32)
    cosB, ssB = tables(64)
    ident = np.eye(128, dtype=np.float32)
    kk = np.arange(128)
    tri = (kk[None, :] >= kk[:, None]).astype(np.float32)
    onehot = np.zeros((8, S), np.float32)
    for n in range(8):
        onehot[n, n * 256:(n + 1) * 256] = 1.0
    biasc = np.full((8, 1024), NEG_BIG, np.float32)
    for qb in range(4):
        biasc[0:qb + 1, qb * 256:(qb + 1) * 256] = 0.0
    return dict(cosA=cosA, ssA=ssA, cosB=cosB, ssB=ssB, ident=ident, tri=tri, onehot=onehot, biasc=biasc)


def make_in_maps(inputs, nseq=2, ncores=NCORES):
    f = lambda a: np.ascontiguousarray(np.asarray(a, dtype=np.float32))
    x = f(inputs["x"])
    gains = np.concatenate([f(inputs["mla_q_norm"])[0], f(inputs["mla_kv_norm"])[0]]).reshape(8, 128).T
    shared = dict(
        w_in0=f(inputs["mla_w_in"])[0], gains=np.ascontiguousarray(gains),
        w_uq=f(inputs["mla_w_uq"])[0], w_ukv=f(inputs["mla_w_ukv"])[0], w_o0=f(inputs["mla_w_o"])[0],
        w_kv=f(inputs["moba_w_kv"]), w_in1=f(inputs["moba_w_in"])[0], w_o1=f(inputs["moba_w_o"])[0],
        ln_g=f(inputs["ln_g"]), ln_b=f(inputs["ln_b"]))
    shared.update(host_consts())
    maps = []
    for c in range(ncores):
        m = dict(shared)
        m["x"] = np.ascontiguousarray(x[c * nseq:(c + 1) * nseq])
        maps.append(m)
    return maps


_NC_CACHE = {}


def kernel(**inputs):
    nseq = 2
    if "nc" not in _NC_CACHE:
        _NC_CACHE["nc"] = build_program(nseq=nseq)
    nc = _NC_CACHE["nc"]
    in_maps = make_in_maps(inputs, nseq=nseq)
    res = run_bass_kernel_spmd(nc, in_maps, core_ids=list(range(NCORES)))
    out = np.concatenate([np.asarray(r["out"], dtype=np.float32) for r in res.results], axis=0)
    return out
```

```python
import numpy as np
from contextlib import ExitStack
import concourse.bass as bass
import concourse.mybir as mybir
from concourse.bass_utils import run_bass_kernel_spmd

F32 = mybir.dt.float32
BF16 = mybir.dt.bfloat16
AF = mybir.ActivationFunctionType
ALU = mybir.AluOpType
AX = mybir.AxisListType

NCORES = 8
S = 2048
D = 1024
NT = S // 128
DEPTH = 2
ALPHA = float((2 * DEPTH) ** 0.25)
LN_EPS = 1e-5
RMS_EPS = 1e-6
Q_LORA, KV_LORA, ROPE_A = 768, 256, 32
SCALE_A = float(96 ** -0.5)
SCALE_B = float(64 ** -0.5)
NEG_BIG = -30000.0
THETA = 10000.0


class Sched:
    def __init__(self, nc, stack):
        self.nc = nc
        self.stack = stack
        self.ops = []
        self.last_w = {}
        self.readers = {}
        self.eng_obj = {"pe": nc.tensor, "act": nc.scalar, "dve": nc.vector,
                        "pool": nc.gpsimd, "sp": nc.sync}

    def add(self, eng, fn, reads=(), writes=(), dma=None):
        idx = len(self.ops)
        deps = {}
        for b in reads:
            p = self.last_w.get(b)
            if p is not None:
                deps[p] = True
            if isinstance(b, tuple) and b[0] == "ps":
                for r in self.readers.get(b, ()):
                    if r not in deps:
                        deps[r] = False
        for b in writes:
            p = self.last_w.get(b)
            if p is not None:
                deps[p] = True
            for r in self.readers.get(b, ()):
                if r not in deps:
                    deps[r] = False
        self.ops.append(dict(eng=eng, fn=fn, deps=deps, dma=dma, signal=False, cnt=None))
        for b in writes:
            self.last_w[b] = idx
            self.readers[b] = []
        for b in reads:
            if b not in writes:
                self.readers.setdefault(b, []).append(idx)
        return idx

    def emit(self, final_wait_engine="sp"):
        nc = self.nc
        ops = self.ops
        need = []
        for i, op in enumerate(ops):
            lst = []
            for p, strong in op["deps"].items():
                po = ops[p]
                if po["dma"] is not None:
                    lst.append(p)
                elif po["eng"] == op["eng"] and op["dma"] is None:
                    if op["eng"] == "pe":
                        continue
                    lst.append(p)
                else:
                    lst.append(p)
            youngest = {}
            keep = []
            for p in lst:
                po = ops[p]
                if po["dma"] is not None:
                    keep.append(p)
                else:
                    e_ = po["eng"]
                    if e_ not in youngest or p > youngest[e_]:
                        youngest[e_] = p
            keep += list(youngest.values())
            lst = keep
            need.append(lst)
            for p in lst:
                ops[p]["signal"] = True
        sem = {}
        cnt = {}

        def get_sem(key):
            if key not in sem:
                sem[key] = self.stack.enter_context(nc.semaphore("s%d" % len(sem)))
                cnt[key] = 0
            return sem[key]

        for op in ops:
            if op["dma"] is not None:
                key = ("dma", op["dma"])
                get_sem(key)
                cnt[key] += 16
                op["cnt"] = cnt[key]
                op["semkey"] = key
            elif op["signal"]:
                key = ("eng", op["eng"])
                get_sem(key)
                cnt[key] += 1
                op["cnt"] = cnt[key]
                op["semkey"] = key
        self.n_sems = len(sem)
        self.max_cnt = dict(cnt)
        seen = {e: {} for e in self.eng_obj}
        done_vc = {}
        self.n_waits = 0
        for i, op in enumerate(ops):
            e = op["eng"]
            eo = self.eng_obj[e]
            waits = {}
            for p in need[i]:
                po = ops[p]
                k = po["semkey"]
                waits[k] = max(waits.get(k, 0), po["cnt"])
            for k, v in waits.items():
                if seen[e].get(k, 0) >= v:
                    continue
                eo.wait_ge(sem[k], v)
                self.n_waits += 1
                seen[e][k] = v
                for k2, v2 in done_vc.get((k, v), {}).items():
                    if seen[e].get(k2, 0) < v2:
                        seen[e][k2] = v2
            ins = op["fn"](eo)
            if op["dma"] is not None:
                ins.then_inc(sem[op["semkey"]], 16)
                vc = dict(seen[e])
                vc[op["semkey"]] = op["cnt"]
                done_vc[(op["semkey"], op["cnt"])] = vc
            elif op["signal"]:
                ins.then_inc(sem[op["semkey"]], 1)
                vc = dict(seen[e])
                vc[op["semkey"]] = op["cnt"]
                done_vc[(op["semkey"], op["cnt"])] = vc
        eo = self.eng_obj[final_wait_engine]
        for k, v in cnt.items():
            if k[0] == "dma" and seen[final_wait_engine].get(k, 0) < v:
                eo.wait_ge(sem[k], v)


class Rot:
    def __init__(self, items):
        self.items = list(items)
        self.i = 0

    def next(self):
        v = self.items[self.i % len(self.items)]
        self.i += 1
        return v


def build_program(nseq=2, layers=(0, 1), dbg_x1=False, max_ops=None):
    nc = bass.Bass("TRN2", target_bir_lowering=False)

    def din(name, shape):
        return nc.dram_tensor(name, list(shape), F32, kind="ExternalInput").ap()

    x_d = din("x", [nseq, S, D])
    w_in0_d = din("w_in0", [D, 2080])
    gains_d = din("gains", [128, 8])
    w_uq_d = din("w_uq", [Q_LORA, 1536])
    w_ukv_d = din("w_ukv", [KV_LORA, 2048])
    w_o0_d = din("w_o0", [D, D])
    w_kv_d = din("w_kv", [D, 2048])
    w_in1_d = din("w_in1", [D, 2048])
    w_o1_d = din("w_o1", [D, D])
    ln_g_d = din("ln_g", [2, D])
    ln_b_d = din("ln_b", [2, D])
    ident_d = din("ident", [128, 128])
    tri_d = din("tri", [128, 128])
    cosA_d = din("cosA", [128, NT, 32])
    ssA_d = din("ssA", [128, NT, 32])
    cosB_d = din("cosB", [128, NT, 64])
    ssB_d = din("ssB", [128, NT, 64])
    onehot_d = din("onehot", [8, S])
    biasc_d = din("biasc", [8, 1024])
    out_d = nc.dram_tensor("out", [nseq, S, D], F32, kind="ExternalOutput").ap()
    if dbg_x1:
        x1_d = nc.dram_tensor("x1dbg", [nseq, S, D], F32, kind="ExternalOutput").ap()
    else:
        x1_d = nc.dram_tensor("x1scr", [nseq, S, D], F32).ap()

    with ExitStack() as st:
        def sb(name, shape, dt):
            return st.enter_context(nc.sbuf_tensor("sb_" + name, list(shape), dt))

        B1 = sb("B1", [128, 8, S], BF16)
        B2 = sb("B2", [128, 8, S], BF16)
        B3 = sb("B3", [128, 8, S], BF16)
        QK = sb("QK", [128, 8192], BF16)
        VG = sb("VG", [128, NT, 3, 64], BF16)
        W1 = sb("W1", [128, 13312], BF16)
        gains = sb("gains", [128, 8], F32)
        ident = sb("ident", [128, 128], BF16)
        tri = sb("tri", [128, 128], BF16)
        ones_bf = sb("ones_bf", [128, 8], BF16)
        mhalf = sb("mhalf", [128, 32], F32)
        tabC = sb("tabC", [128, NT * 64], F32)
        tabS = sb("tabS", [128, NT * 64], F32)
        cosA = tabC[:, 0:NT * 32].rearrange("p (t d) -> p t d", t=NT)
        ssA = tabS[:, 0:NT * 32].rearrange("p (t d) -> p t d", t=NT)
        cosB = tabC[:, :].rearrange("p (t d) -> p t d", t=NT)
        ssB = tabS[:, :].rearrange("p (t d) -> p t d", t=NT)
        g_bc = sb("g_bc", [128, D], F32)
        b_bc = sb("b_bc", [128, D], F32)
        xb = [sb("xb%d" % i, [128, D], BF16) for i in range(2)]
        xres = [sb("xres%d" % i, [128, D], F32) for i in range(4)]
        sq = [sb("sq%d" % i, [128, 512], BF16) for i in range(2)]
        PT = [sb("PT%d" % i, [128, 1024], BF16) for i in range(4)]
        q32 = [sb("q32_%d" % i, [128, 256], F32) for i in range(2)]
        kv32 = [sb("kv32_%d" % i, [128, 256], F32) for i in range(2)]
        Qa = [sb("Qa%d" % i, [128, 2, 96], BF16) for i in range(3)]
        Ka = [sb("Ka%d" % i, [128, 2, 96], BF16) for i in range(3)]
        rt1 = [sb("rt1_%d" % i, [128, 2, 64], F32) for i in range(2)]
        rt2 = [sb("rt2_%d" % i, [128, 2, 64], F32) for i in range(2)]
        krope = sb("krope", [128, NT, 32], BF16)
        wkr = sb("wkr", [128, 8, 32], BF16)
        rstd = sb("rstd", [128, 32], F32)
        rg = [sb("rg%d" % i, [128, 512], F32) for i in range(2)]
        kr1 = rg[0][:, :].rearrange("p (t d) -> p t d", t=NT)
        kr2 = rg[1][:, :].rearrange("p (t d) -> p t d", t=NT)
        gsv = [sb("gsv%d" % i, [128, 2, 8], F32) for i in range(2)]
        cmpt = [sb("cmp%d" % i, [128, 2, 8, 8], F32) for i in range(2)]
        rank = [sb("rank%d" % i, [128, 2, 8], F32) for i in range(2)]
        bias = [sb("bias%d" % i, [128, 2, 8], BF16) for i in range(4)]
        km = sb("km", [64, 2, 8], F32)
        kmT = sb("kmT", [64, 2, 8], BF16)
        lnst = [sb("lnst%d" % i, [128, 2, 6], F32) for i in range(4)]
        lnmv = [sb("lnmv%d" % i, [128, 2], F32) for i in range(4)]
        lnr = [sb("lnr%d" % i, [128, 2], F32) for i in range(4)]
        PS = st.enter_context(nc.psum_tensor("PS", [128, 8, 512], F32))
        ps = [PS[:, i, :] for i in range(8)]

        def psb(i):
            return ps[i].bitcast(BF16)

        qT = QK[0:96, 0:4096].rearrange("p (h s) -> p h s", h=2)
        kT = QK[0:96, 4096:8192].rearrange("p (h s) -> p h s", h=2)
        w2 = [QK[:, i * 4096:(i + 1) * 4096].rearrange("p (k n) -> p k n", k=8) for i in range(2)]
        W1a = W1[:, 0:9216]
        W1b = W1[:, 9216:13312]
        w_uq_sb = W1a.rearrange("p (f n) -> p f n", f=6)
        w_ukv_sb = W1b.rearrange("p (f n) -> p f n", f=2)
        w_o_sb = W1[:, 0:8192].rearrange("p (k n) -> p k n", k=8)
        wg = [W1[:, 0:3072].rearrange("p (k n) -> p k n", k=8),
              W1[:, 9216:12288].rearrange("p (k n) -> p k n", k=8)]

        SC = Sched(nc, st)
        MARKS = {}

        def mark(name):
            MARKS[name] = len(SC.ops)

        def op(eng, meth, *args, reads=(), writes=(), dma=None, **kw):
            SC.add(eng, (lambda e: getattr(e, meth)(*args, **kw)), reads=reads, writes=writes, dma=dma)

        def kB(name, cs, ts):
            return [(name, c, t) for c in cs for t in ts]

        def kB1(cs, ts):
            return [("B1", c, t, hl) for c in cs for t in ts for hl in range(2)]
        R8 = range(8)
        RT = range(NT)
        K_QT = [("qT", h, t) for h in range(2) for t in RT]
        K_KT = [("kT", h, t) for h in range(2) for t in RT]
        K_W2 = [K_QT, K_KT]
        K_W1A = [("w1", 0)]
        K_W1B = [("w1", 1)]

        def kps(i):
            return [("ps", i)]

        def wT(w_d):
            return w_d.rearrange("(kc p) n -> p kc n", p=128)

        def tslice(t):
            return slice(t * 128, (t + 1) * 128)

        op("pool", "dma_start", out=ident[:], in_=ident_d, writes=["ident"], dma="ident")
        op("pool", "dma_start", out=tri[:], in_=tri_d, writes=["tri"], dma="tri")
        op("sp", "dma_start", out=gains[:], in_=gains_d, writes=["gains"], dma="gains")
        op("dve", "memset", ones_bf[:], 1.0, writes=["ones_bf"])
        op("dve", "memset", mhalf[:], -0.5, writes=["mhalf"])
        op("dve", "memset", VG[:, :, 1, :], 1.0, writes=["VG1"])

        gen = Rot([6, 7])

        def pipeline(stages, n):
            for s_ in range(n + len(stages) - 1):
                for k_ in range(len(stages) - 1, -1, -1):
                    t_ = s_ - k_
                    if 0 <= t_ < n:
                        stages[k_](t_)
        pt_rot = Rot([0, 1, 2, 3])
        rg_rot = Rot([0, 1])

        def preload_c(layer, w_o_d):
            op("pool", "dma_start", out=w_o_sb, in_=wT(w_o_d), writes=K_W1A, dma="w1a")
            load_ln(layer)

        def load_ln(layer):
            op("sp", "dma_start", out=g_bc[:], in_=ln_g_d[layer:layer + 1, :].to_broadcast([128, D]),
               writes=["g_bc"], dma="g_bc")
            op("sp", "dma_start", out=b_bc[:], in_=ln_b_d[layer:layer + 1, :].to_broadcast([128, D]),
               writes=["b_bc"], dma="b_bc")

        def transposes_to(src_bf, srckey, DST, dkeys, t, evac_eng, bank=None):
            b = gen.next() if bank is None else bank
            pv = psb(b)
            for kc in range(8):
                op("pe", "transpose", pv[:, kc * 128:(kc + 1) * 128], src_bf[:, kc * 128:(kc + 1) * 128], ident[:],
                   reads=[srckey, "ident"], writes=kps(b))
            src = pv[:, 0:1024].rearrange("p (c n) -> p c n", c=8)
            if evac_eng == "dve":
                op("dve", "tensor_copy", DST[:, :, tslice(t)], src, reads=kps(b), writes=dkeys)
            else:
                op("act", "copy", DST[:, :, tslice(t)], src, reads=kps(b), writes=dkeys)

        def attention_group(g, Kd, scale, L=3):
            slot = Rot([0, 1, 3])
            otb = Rot([4, 5])
            units = []
            for hl in range(2):
                for qc in range(4):
                    ob = otb.next()
                    nj = 4 * qc + 4
                    for j in range(0, 4 * qc, 2):
                        units.append(dict(hl=hl, qc=qc, js=[j, j + 1], nj=nj, ob=ob, diag=False))
                    for j in range(4 * qc, nj):
                        units.append(dict(hl=hl, qc=qc, js=[j], nj=nj, ob=ob, diag=True))

            def front(u):
                hl, qc = u["hl"], u["qc"]
                sl = slot.next()
                pi = pt_rot.next()
                if not u["diag"]:
                    qlo, n = qc * 512, 512
                    tq = list(range(qc * 4, qc * 4 + 4))
                    for k, j in enumerate(u["js"]):
                        bank = 2 * sl + k
                        op("pe", "matmul", ps[bank][:, 0:512], kT[0:Kd, hl, j * 128:(j + 1) * 128],
                           qT[0:Kd, hl, qlo:qlo + 512], start=True, stop=True,
                           reads=[("kT", hl, j)] + [("qT", hl, t) for t in tq], writes=kps(bank))
                    op("act", "activation", PT[pi][:, :].rearrange("p (b n) -> p b n", b=2),
                       PS[:, 2 * sl:2 * sl + 2, :], AF.Exp, scale=scale,
                       reads=kps(2 * sl) + kps(2 * sl + 1), writes=[("PT", pi)])
                else:
                    j = u["js"][0]
                    qlo = j * 128
                    n = (qc + 1) * 512 - qlo
                    bank = 2 * sl
                    tq = list(range(j, (qc + 1) * 4))
                    op("pe", "matmul", ps[bank][:, 0:n], kT[0:Kd, hl, j * 128:(j + 1) * 128],
                       qT[0:Kd, hl, qlo:qlo + n], start=True, stop=True,
                       reads=[("kT", hl, j)] + [("qT", hl, t) for t in tq], writes=kps(bank))
                    op("act", "activation", PT[pi][:, 0:n], ps[bank][:, 0:n], AF.Exp, scale=scale,
                       reads=kps(bank), writes=[("PT", pi)])
                    op("pool", "tensor_tensor", PT[pi][:, 0:128], PT[pi][:, 0:128], tri[:], ALU.mult,
                       reads=[("PT", pi), "tri"], writes=[("PT", pi)])
                u.update(pi=pi, qlo=qlo, n=n)

            def back(u):
                hl, qc, nj, ob, pi, n, qlo = (u[k] for k in ("hl", "qc", "nj", "ob", "pi", "n", "qlo"))
                o0 = qlo - qc * 512
                for k, j in enumerate(u["js"]):
                    vT = VG[:, j, hl:hl + 2, :].rearrange("p a d -> p (a d)")
                    op("pe", "matmul", ps[ob][:, o0:512], vT, PT[pi][:, k * 512:k * 512 + n],
                       start=(j == 0), stop=(j == nj - 1),
                       reads=[("PT", pi), ("VG", hl, j), "VG1"], writes=kps(ob))
                if u["js"][-1] == nj - 1:
                    p0 = hl * 64
                    p1 = 64 - p0
                    ri = rg_rot.next()
                    cols = slice(qc * 512, (qc + 1) * 512)
                    ts4 = list(range(qc * 4, qc * 4 + 4))
                    op("dve", "reciprocal", rg[ri][p0:p0 + 64, :], ps[ob][p1:p1 + 64, :],
                       reads=kps(ob), writes=[("rg", ri)])
                    op("dve", "tensor_tensor", rg[ri][p0:p0 + 64, :], rg[ri][p0:p0 + 64, :],
                       B2[p0:p0 + 64, g, cols], ALU.mult,
                       reads=[("rg", ri)] + kB("B2", [g], ts4), writes=[("rg", ri)])
                    op("dve", "tensor_tensor", B1[p0:p0 + 64, g, cols], ps[ob][p0:p0 + 64, :],
                       rg[ri][p0:p0 + 64, :], ALU.mult,
                       reads=kps(ob) + [("rg", ri)], writes=[("B1", g, t, hl) for t in ts4])

            for i in range(len(units) + L):
                if i < len(units):
                    front(units[i])
                if i >= L:
                    back(units[i - L])

        def phase_c(s, layer, src_d, dst_d, to_B3, w_o_d, pre_fn=None, tail_stage=None):
            if pre_fn is not None:
                pre_fn()
            ybanks = Rot([0, 1, 2, 3, 4, 5])
            ybk = {}

            def c1(t):
                i = t % 4
                tok = tslice(t)
                xr = xres[i]
                kx = [("xres", i)]
                op("sp", "dma_start", out=xr[:], in_=src_d[s, tok, :],
                   reads=[("xsrc", layer, s, t)], writes=kx, dma=("xres", i))
                yb = [ybanks.next(), ybanks.next()]
                ybk[t] = yb
                for half in range(2):
                    for c in range(8):
                        op("pe", "matmul", ps[yb[half]][:, :], B1[:, c, tok],
                           w_o_sb[:, c, half * 512:(half + 1) * 512], start=(c == 0), stop=(c == 7),
                           reads=kB1([c], [t]) + K_W1A, writes=kps(yb[half]))

            def c2(t):
                i = t % 4
                xr = xres[i]
                kx = [("xres", i)]
                yb = ybk[t]
                for half in range(2):
                    hs = slice(half * 512, (half + 1) * 512)
                    op("dve", "scalar_tensor_tensor", xr[:, hs], xr[:, hs], ALPHA, ps[yb[half]][:, :],
                       ALU.mult, ALU.add, reads=kx + kps(yb[half]), writes=kx)
                    op("dve", "bn_stats", lnst[i][:, half, :], xr[:, hs], reads=kx, writes=[("lnst", i)])
                op("dve", "bn_aggr", lnmv[i][:], lnst[i][:].rearrange("p a b -> p (a b)"),
                   reads=[("lnst", i)], writes=[("lnmv", i)])
                op("dve", "tensor_scalar", lnr[i][:, 0:1], lnmv[i][:, 1:2], LN_EPS, None, ALU.add,
                   reads=[("lnmv", i)], writes=[("lnr", i)])
                op("pool", "tensor_tensor", lnr[i][:, 0:1], lnr[i][:, 0:1], mhalf[:, 0:1], ALU.pow,
                   reads=[("lnr", i), "mhalf"], writes=[("lnr", i)])

            def c3(t):
                i = t % 4
                tok = tslice(t)
                xr = xres[i]
                kx = [("xres", i)]
                op("dve", "scalar_tensor_tensor", xr[:], xr[:], lnmv[i][:, 0:1], g_bc[:], ALU.subtract, ALU.mult,
                   reads=kx + [("lnmv", i), "g_bc"], writes=kx)
                op("act", "activation", xr[:], xr[:], AF.Identity, scale=lnr[i][:, 0:1],
                   reads=kx + [("lnr", i)], writes=kx)

            def c3b(t):
                i = t % 4
                tok = tslice(t)
                xr = xres[i]
                kx = [("xres", i)]
                op("pool", "tensor_tensor", xr[:], xr[:], b_bc[:], ALU.add, reads=kx + ["b_bc"], writes=kx)
                op("sp", "dma_start", out=dst_d[s, tok, :], in_=xr[:],
                   reads=kx, writes=[("xsrc", layer + 1, s, t)], dma=("xres", i))
                if to_B3:
                    j = t % 2
                    op("act", "copy", xb[j][:], xr[:], reads=kx, writes=[("xb", j)])

            c4b = {}

            def c4(t):
                j = t % 2
                b = gen.next()
                c4b[t] = b
                pv = psb(b)
                for kc in range(8):
                    op("pe", "transpose", pv[:, kc * 128:(kc + 1) * 128], xb[j][:, kc * 128:(kc + 1) * 128], ident[:],
                       reads=[("xb", j), "ident"], writes=kps(b))

            def c5(t):
                b = c4b[t]
                pv = psb(b)
                op("act", "copy", B3[:, :, tslice(t)], pv[:, 0:1024].rearrange("p (c n) -> p c n", c=8),
                   reads=kps(b), writes=kB("B3", R8, [t]))

            stages = [c1, c2, c3, c3b] + ([c4, c5] if to_B3 else []) + ([tail_stage] if tail_stage is not None else [])
            pipeline(stages, NT)

        def rope_ops(i, src3, cos_t, ss_t, hd, dst3, dstkey, srckeys, ck, sk):
            hh = hd // 2
            op("dve", "tensor_tensor", rt1[i][:, :, 0:hd], src3, cos_t.to_broadcast([128, 2, hd]), ALU.mult,
               reads=srckeys + [ck], writes=[("rt1", i)])
            op("dve", "tensor_tensor", rt2[i][:, :, 0:hh], src3[:, :, hh:hd],
               ss_t[:, :, 0:hh].to_broadcast([128, 2, hh]), ALU.mult,
               reads=srckeys + [sk], writes=[("rt2", i)])
            op("dve", "tensor_tensor", rt2[i][:, :, hh:hd], src3[:, :, 0:hh],
               ss_t[:, :, hh:hd].to_broadcast([128, 2, hh]), ALU.mult,
               reads=srckeys + [sk], writes=[("rt2", i)])
            op("dve", "tensor_tensor", dst3, rt1[i][:, :, 0:hd], rt2[i][:, :, 0:hd], ALU.add,
               reads=[("rt1", i), ("rt2", i)], writes=[dstkey])

        l0_done = set()

        def l0_prefetch(s):
            op("pool", "dma_start", out=w2[0], in_=wT(w_in0_d)[:, :, 0:512], writes=K_W2[0], dma=("w2", 0))
            op("pool", "dma_start", out=wkr[:], in_=wT(w_in0_d)[:, :, 1024:1056], writes=["wkr"], dma="wkr")

        def l0_a0(s, t, bank=6):
            i = t % 2
            op("pool", "dma_start", out=xb[i][:], in_=x_d[s, tslice(t), :], writes=[("xb", i)], dma=("xb", i))
            transposes_to(xb[i], ("xb", i), B1, kB1(R8, [t]), t, "dve" if t % 2 == 0 else "act", bank=bank)

        def l1_prefetch():
            for sl in range(2):
                op("pool", "dma_start", out=w2[sl], in_=wT(w_in1_d)[:, :, 1024 + sl * 512:1024 + (sl + 1) * 512],
                   writes=K_W2[sl], dma=("w2", sl))

        def layer0(s):
            op("sp", "dma_start", out=cosA, in_=cosA_d, writes=["tabC"], dma="tabC")
            op("sp", "dma_start", out=ssA, in_=ssA_d, writes=["tabS"], dma="tabS")
            col0 = [0, 512, 1056, 1568]
            prefetched = s in l0_done
            if not prefetched:
                l0_prefetch(s)

            def a0(t):
                if not prefetched:
                    l0_a0(s, t)

            for t in range(4):
                a0(t)
            op("pool", "dma_start", out=w_uq_sb, in_=w_uq_d.rearrange("(f p) n -> p f n", p=128),
               writes=K_W1A, dma="w1a")
            op("pool", "dma_start", out=w_ukv_sb, in_=w_ukv_d.rearrange("(f p) n -> p f n", p=128),
               writes=K_W1B, dma="w1b")
            mark('A0_end')
            op("dve", "memset", ps[7][:, 0:32], 0.0, writes=kps(7))
            banks = Rot([0, 1, 2, 3, 4, 5])
            sqr = Rot([0, 1])
            pend = []

            def flush_stats(keep):
                while len(pend) > keep:
                    si_, c4_, f_ = pend.pop(0)
                    for tt in range(4):
                        col = (c4_ * 4 + tt) * 2 + (0 if f_ < 6 else 1)
                        op("pe", "matmul", ps[7][:, col:col + 1], sq[si_][:, tt * 128:(tt + 1) * 128],
                           ones_bf[:, 0:1], start=False, stop=False, skip_group_check=True,
                           reads=[("sq", si_), "ones_bf"], writes=kps(7))
            for sl in range(4):
                wi = sl % 2
                if sl > 0:
                    op("pool", "dma_start", out=w2[wi], in_=wT(w_in0_d)[:, :, col0[sl]:col0[sl] + 512],
                       writes=K_W2[wi], dma=("w2", wi))
                for c4 in range(4):
                    cols = slice(c4 * 512, (c4 + 1) * 512)
                    ts4 = list(range(c4 * 4, c4 * 4 + 4))
                    if sl == 0 and c4 < 3:
                        for t_ in range(4 * c4 + 4, 4 * c4 + 8):
                            a0(t_)
                    for j in range(4):
                        b = banks.next()
                        for kc in range(8):
                            op("pe", "matmul", ps[b][:, :], w2[wi][:, kc, j * 128:(j + 1) * 128], B1[:, kc, cols],
                               start=(kc == 0), stop=(kc == 7),
                               reads=K_W2[wi] + kB1([kc], ts4), writes=kps(b))
                        flush_stats(1)
                        if sl < 2:
                            f = sl * 4 + j
                            op("dve", "tensor_scalar", B3[:, f, cols], ps[b][:, :], gains[:, f:f + 1], None, ALU.mult,
                               reads=kps(b) + ["gains"], writes=kB("B3", [f], ts4))
                            si = sqr.next()
                            op("act", "activation", sq[si][:], ps[b][:, :], AF.Square,
                               reads=kps(b), writes=[("sq", si)])
                            pend.append((si, c4, f))
                        else:
                            f = (sl - 2) * 4 + j
                            op("act", "activation", B2[:, f, cols], ps[b][:, :], AF.Silu,
                               reads=kps(b), writes=kB("B2", [f], ts4))
            flush_stats(0)
            mark('A1_slabs_end')
            op("dve", "tensor_scalar", rstd[:, 0:32:2], ps[7][:, 0:32:2], 1.0 / Q_LORA, RMS_EPS, ALU.mult, ALU.add,
               reads=kps(7), writes=["rstd"])
            op("dve", "tensor_scalar", rstd[:, 1:32:2], ps[7][:, 1:32:2], 1.0 / KV_LORA, RMS_EPS, ALU.mult, ALU.add,
               reads=kps(7), writes=["rstd"])
            op("pool", "tensor_tensor", rstd[:], rstd[:], mhalf[:], ALU.pow, reads=["rstd", "mhalf"], writes=["rstd"])
            for t in RT:
                for kc in range(8):
                    op("pe", "matmul", ps[6][:, t * 32:(t + 1) * 32], B1[:, kc, tslice(t)], wkr[:, kc, :],
                       start=(kc == 0), stop=(kc == 7), reads=kB1([kc], [t]) + ["wkr"], writes=kps(6))
            p6 = ps[6][:, :].rearrange("p (t d) -> p t d", t=NT)
            op("dve", "tensor_tensor", kr1, p6, cosA, ALU.mult, reads=kps(6) + ["tabC"], writes=[("rg", 0)])
            op("dve", "tensor_tensor", kr2[:, :, 0:16], p6[:, :, 16:32], ssA[:, :, 0:16], ALU.mult,
               reads=kps(6) + ["tabS"], writes=[("rg", 1)])
            op("dve", "tensor_tensor", kr2[:, :, 16:32], p6[:, :, 0:16], ssA[:, :, 16:32], ALU.mult,
               reads=kps(6) + ["tabS"], writes=[("rg", 1)])
            op("dve", "tensor_tensor", krope[:], kr1, kr2, ALU.add, reads=[("rg", 0), ("rg", 1)], writes=["krope"])
            mark('A1_end')
            pa_rot = Rot([0, 1, 2, 3])
            pc_rot = Rot([6, 7])
            for g in range(8):
                st_ = {}
                sc_ = {}

                def s0(t, g=g, st_=st_):
                    tok = tslice(t)
                    b = pa_rot.next()
                    st_[t] = b
                    for f in range(6):
                        op("pe", "matmul", ps[b][:, 0:192], B3[:, f, tok], w_uq_sb[:, f, g * 192:(g + 1) * 192],
                           start=(f == 0), stop=(f == 5), reads=kB("B3", [f], [t]) + K_W1A, writes=kps(b))
                    for f in range(2):
                        op("pe", "matmul", ps[b][:, 256:512], B3[:, 6 + f, tok], w_ukv_sb[:, f, g * 256:(g + 1) * 256],
                           start=(f == 0), stop=(f == 1), reads=kB("B3", [6 + f], [t]) + K_W1B, writes=kps(b))

                def s1(t, st_=st_):
                    i = t % 2
                    b = st_[t]
                    op("act", "activation", q32[i][:, 0:192], ps[b][:, 0:192], AF.Identity,
                       scale=rstd[:, 2 * t:2 * t + 1], reads=kps(b) + ["rstd"], writes=[("q32", i)])
                    op("act", "activation", kv32[i][:, :], ps[b][:, 256:512], AF.Identity,
                       scale=rstd[:, 2 * t + 1:2 * t + 2], reads=kps(b) + ["rstd"], writes=[("kv32", i)])

                def s2(t):
                    i = t % 2
                    j = t % 3
                    hd, hh = 32, 16
                    qv = q32[i][:, 0:192].rearrange("p (h d) -> p h d", h=2)
                    kvv = kv32[i][:, :].rearrange("p (h d) -> p h d", h=2)
                    src3 = qv[:, :, 64:96]
                    op("dve", "tensor_tensor", rt1[i][:, :, 0:hd], src3, cosA[:, t:t + 1, :].to_broadcast([128, 2, hd]), ALU.mult,
                       reads=[("q32", i), "tabC"], writes=[("rt1", i)])
                    op("dve", "tensor_tensor", rt2[i][:, :, 0:hh], src3[:, :, hh:hd],
                       ssA[:, t:t + 1, 0:hh].to_broadcast([128, 2, hh]), ALU.mult,
                       reads=[("q32", i), "tabS"], writes=[("rt2", i)])
                    op("dve", "tensor_tensor", rt2[i][:, :, hh:hd], src3[:, :, 0:hh],
                       ssA[:, t:t + 1, hh:hd].to_broadcast([128, 2, hh]), ALU.mult,
                       reads=[("q32", i), "tabS"], writes=[("rt2", i)])
                    op("pool", "tensor_copy", Qa[j][:, :, 0:64], qv[:, :, 0:64], reads=[("q32", i)], writes=[("Qa", j)])
                    op("pool", "tensor_copy", Ka[j][:, :, 0:64], kvv[:, :, 0:64], reads=[("kv32", i)], writes=[("Ka", j)])
                    op("pool", "tensor_copy", Ka[j][:, :, 64:96], krope[:, t:t + 1, :].to_broadcast([128, 2, 32]),
                       reads=["krope"], writes=[("Ka", j)])
                    op("pool", "tensor_copy", VG[:, t, 0:3:2, :], kvv[:, :, 64:128],
                       reads=[("kv32", i)], writes=[("VG", 0, t), ("VG", 1, t)])

                def s3(t):
                    i = t % 2
                    j = t % 3
                    op("dve", "tensor_tensor", Qa[j][:, :, 64:96], rt1[i][:, :, 0:32], rt2[i][:, :, 0:32], ALU.add,
                       reads=[("rt1", i), ("rt2", i)], writes=[("Qa", j)])

                def s4(t, sc_=sc_):
                    j = t % 3
                    b2 = pc_rot.next()
                    sc_[t] = b2
                    pv = psb(b2)
                    for hl in range(2):
                        op("pe", "transpose", pv[0:96, hl * 128:(hl + 1) * 128], Qa[j][:, hl, :], ident[:],
                           reads=[("Qa", j), "ident"], writes=kps(b2))
                    for hl in range(2):
                        op("pe", "transpose", pv[0:96, 256 + hl * 128:256 + (hl + 1) * 128], Ka[j][:, hl, :], ident[:],
                           reads=[("Ka", j), "ident"], writes=kps(b2))

                def s5(t, sc_=sc_):
                    tok = tslice(t)
                    b2 = sc_[t]
                    pv = psb(b2)
                    dst = QK[0:96, :].rearrange("p (a s) -> p a s", a=4)[:, :, tok]
                    srcv = pv[0:96, 0:512].rearrange("p (a n) -> p a n", a=4)
                    wk = [("qT", 0, t), ("qT", 1, t), ("kT", 0, t), ("kT", 1, t)]
                    if t % 2 == 0:
                        op("act", "copy", dst, srcv, reads=kps(b2), writes=wk)
                    else:
                        op("dve", "tensor_copy", dst, srcv, reads=kps(b2), writes=wk)

                pipeline([s0, s1, s2, s3, s4, s5], NT)
                mark('L0_g%d_proj_end' % g)
                if g == 7:
                    preload_c(0, w_o0_d)
                attention_group(g, 96, SCALE_A)
                mark('L0_g%d_att_end' % g)
            phase_c(s, 0, x_d, x1_d, True, w_o0_d, pre_fn=l1_prefetch)
            mark('L0_end')

        def layer1(s):
            op("sp", "dma_start", out=cosB, in_=cosB_d, writes=["tabC"], dma="tabC")
            op("sp", "dma_start", out=ssB, in_=ssB_d, writes=["tabS"], dma="tabS")
            def load_wg(g):
                gi = g % 2
                key = K_W1A if gi == 0 else K_W1B
                op("pool", "dma_start", out=wg[gi][:, :, 0:128], in_=wT(w_in1_d)[:, :, g * 128:(g + 1) * 128],
                   writes=key, dma=("wg", gi))
                op("pool", "dma_start", out=wg[gi][:, :, 128:256], in_=wT(w_kv_d)[:, :, g * 128:(g + 1) * 128],
                   writes=key, dma=("wg", gi))
                op("pool", "dma_start", out=wg[gi][:, :, 256:384],
                   in_=wT(w_kv_d)[:, :, 1024 + g * 128:1024 + (g + 1) * 128], writes=key, dma=("wg", gi))

            load_wg(0)
            banks = Rot([0, 1, 2, 3, 4, 5])
            for sl in range(2):
                wi = sl % 2
                for c4 in range(4):
                    cols = slice(c4 * 512, (c4 + 1) * 512)
                    ts4 = list(range(c4 * 4, c4 * 4 + 4))
                    for j in range(4):
                        b = banks.next()
                        f = sl * 4 + j
                        for kc in range(8):
                            op("pe", "matmul", ps[b][:, :], w2[wi][:, kc, j * 128:(j + 1) * 128], B3[:, kc, cols],
                               start=(kc == 0), stop=(kc == 7),
                               reads=K_W2[wi] + kB("B3", [kc], ts4), writes=kps(b))
                        op("act", "activation", B2[:, f, cols], ps[b][:, :], AF.Silu,
                           reads=kps(b), writes=kB("B2", [f], ts4))
            for hl in range(2):
                op("pool", "dma_start", out=kT[64:72, hl, :], in_=onehot_d,
                   writes=[("kT", hl, t) for t in RT], dma=("oh", hl))
                op("pool", "dma_start", out=qT[64:72, hl, 0:1024], in_=biasc_d,
                   writes=[("qT", hl, t) for t in range(8)], dma=("oh", hl))

            for g in range(8):
                gi = g % 2
                wkey = K_W1A if gi == 0 else K_W1B
                if g + 1 < 8:
                    load_wg(g + 1)
                if g == 7:
                    preload_c(1, w_o1_d)
                pa_rot = Rot([0, 1, 2, 3])
                pc_rot = Rot([4, 5])
                pd_rot = Rot([6, 7])
                stk = {}
                stc = {}

                def p0_(t, gi=gi, wkey=wkey, stk=stk):
                    tok = tslice(t)
                    b = pa_rot.next()
                    stk[t] = b
                    for kc in range(8):
                        op("pe", "matmul", ps[b][:, 0:384], B3[:, kc, tok], wg[gi][:, kc, 0:384],
                           start=(kc == 0), stop=(kc == 7), reads=kB("B3", [kc], [t]) + wkey, writes=kps(b))

                def p1_(t, stk=stk):
                    i = t % 2
                    b = stk[t]
                    src4 = ps[b][:, 0:256].rearrange("p (h d) -> p h d", h=4)
                    r1 = q32[i][:, :].rearrange("p (h d) -> p h d", h=4)
                    r2 = kv32[i][:, :].rearrange("p (h d) -> p h d", h=4)
                    op("dve", "tensor_tensor", r1, src4, cosB[:, t:t + 1, :].to_broadcast([128, 4, 64]), ALU.mult,
                       reads=kps(b) + ["tabC"], writes=[("q32", i)])
                    op("dve", "tensor_tensor", r2[:, :, 0:32], src4[:, :, 32:64],
                       ssB[:, t:t + 1, 0:32].to_broadcast([128, 4, 32]), ALU.mult,
                       reads=kps(b) + ["tabS"], writes=[("kv32", i)])
                    op("dve", "tensor_tensor", r2[:, :, 32:64], src4[:, :, 0:32],
                       ssB[:, t:t + 1, 32:64].to_broadcast([128, 4, 32]), ALU.mult,
                       reads=kps(b) + ["tabS"], writes=[("kv32", i)])
                    op("act", "copy", VG[:, t, 0:3:2, :], ps[b][:, 256:384].rearrange("p (h d) -> p h d", h=2),
                       reads=kps(b), writes=[("VG", 0, t), ("VG", 1, t)])

                def p2_(t):
                    i = t % 2
                    j = t % 3
                    r1 = q32[i][:, :].rearrange("p (h d) -> p h d", h=4)
                    r2 = kv32[i][:, :].rearrange("p (h d) -> p h d", h=4)
                    op("pool", "tensor_tensor", Qa[j][:, :, 0:64], r1[:, 0:2, :], r2[:, 0:2, :], ALU.add,
                       reads=[("q32", i), ("kv32", i)], writes=[("Qa", j)])
                    op("pool", "tensor_tensor", Ka[j][:, :, 0:64], r1[:, 2:4, :], r2[:, 2:4, :], ALU.add,
                       reads=[("q32", i), ("kv32", i)], writes=[("Ka", j)])

                def p3_(t, stc=stc):
                    j = t % 3
                    b2 = pc_rot.next()
                    stc[t] = b2
                    pv = psb(b2)
                    for hl in range(2):
                        op("pe", "transpose", pv[0:64, hl * 128:(hl + 1) * 128], Qa[j][:, hl, 0:64], ident[:],
                           reads=[("Qa", j), "ident"], writes=kps(b2))
                    for hl in range(2):
                        op("pe", "transpose", pv[0:64, 256 + hl * 128:256 + (hl + 1) * 128], Ka[j][:, hl, 0:64], ident[:],
                           reads=[("Ka", j), "ident"], writes=kps(b2))

                def p4_(t, stc=stc):
                    tok = tslice(t)
                    b2 = stc[t]
                    pv = psb(b2)
                    dst = QK[0:64, :].rearrange("p (a s) -> p a s", a=4)[:, :, tok]
                    srcv = pv[0:64, 0:512].rearrange("p (a n) -> p a n", a=4)
                    wk = [("qT", 0, t), ("qT", 1, t), ("kT", 0, t), ("kT", 1, t)]
                    if t % 2 == 0:
                        op("act", "copy", dst, srcv, reads=kps(b2), writes=wk)
                    else:
                        op("dve", "tensor_copy", dst, srcv, reads=kps(b2), writes=wk)

                pipeline([p0_, p1_, p2_, p3_, p4_], NT)
                for hl in range(2):
                    op("dve", "tensor_reduce", km[:, hl, :], kT[0:64, hl, :].rearrange("p (n k) -> p n k", n=8),
                       AX.X, ALU.add, reads=[("kT", hl, t) for t in RT], writes=["km"])
                op("dve", "tensor_scalar", kmT[:], km[:], 1.0 / 256.0, None, ALU.mult, reads=["km"], writes=["kmT"])
                sg = {}
                sd = {}

                def t0_(t0, sg=sg):
                    t = 8 + t0
                    bi = t % 4
                    qb = t // 2
                    tok = tslice(t)
                    b3 = pc_rot.next()
                    sg[t] = b3
                    for hl in range(2):
                        op("pe", "matmul", ps[b3][:, hl * 8:(hl + 1) * 8], qT[0:64, hl, tok], kmT[:, hl, :],
                           start=True, stop=True, reads=[("qT", hl, t), "kmT"], writes=kps(b3))
                    op("pool", "memset", bias[bi][:], NEG_BIG, writes=[("bias", bi)])
                    op("pool", "memset", bias[bi][:, :, qb:qb + 1], 0.0, writes=[("bias", bi)])

                def t1_(t0, sg=sg):
                    t = 8 + t0
                    i = t % 2
                    bi = t % 4
                    qb = t // 2
                    b3 = sg[t]
                    op("dve", "tensor_copy", gsv[i][:], ps[b3][:, 0:16].rearrange("p (h n) -> p h n", h=2),
                       reads=kps(b3), writes=[("gsv", i)])
                    op("dve", "tensor_tensor", cmpt[i][:, :, 0:qb, 0:qb],
                       gsv[i][:, :, 0:qb].unsqueeze(2).to_broadcast([128, 2, qb, qb]),
                       gsv[i][:, :, 0:qb].unsqueeze(3).to_broadcast([128, 2, qb, qb]), ALU.is_gt,
                       reads=[("gsv", i)], writes=[("cmp", i)])
                    op("dve", "tensor_reduce", rank[i][:, :, 0:qb], cmpt[i][:, :, 0:qb, 0:qb], AX.X, ALU.add,
                       reads=[("cmp", i)], writes=[("rank", i)])
                    op("dve", "tensor_scalar", bias[bi][:, :, 0:qb], rank[i][:, :, 0:qb], 3.0, NEG_BIG,
                       ALU.is_ge, ALU.mult, reads=[("rank", i)], writes=[("bias", bi)])

                def t2_(t0, sd=sd):
                    t = 8 + t0
                    bi = t % 4
                    b4 = pd_rot.next()
                    sd[t] = b4
                    pv4 = psb(b4)
                    for hl in range(2):
                        op("pe", "transpose", pv4[0:8, hl * 128:(hl + 1) * 128], bias[bi][:, hl, :], ident[:],
                           reads=[("bias", bi), "ident"], writes=kps(b4))

                def t3_(t0, sd=sd):
                    t = 8 + t0
                    tok = tslice(t)
                    pv4 = psb(sd[t])
                    op("act", "copy", qT[64:72, :, tok], pv4[0:8, 0:256].rearrange("p (h n) -> p h n", h=2),
                       reads=kps(sd[t]), writes=[("qT", 0, t), ("qT", 1, t)])

                pipeline([t0_, t1_, t2_, t3_], 8)
                attention_group(g, 72, SCALE_B)
            if s + 1 < nseq:
                l0_done.add(s + 1)
                phase_c(s, 1, x1_d, out_d, False, w_o1_d, pre_fn=(lambda: l0_prefetch(s + 1)),
                        tail_stage=(lambda t: l0_a0(s + 1, t, bank=6 + (t % 2))))
            else:
                phase_c(s, 1, x1_d, out_d, False, w_o1_d)

        for s in range(nseq):
            if 0 in layers:
                layer0(s)
            if 1 in layers:
                layer1(s)
        build_program.marks = dict(MARKS)
        if max_ops is not None:
            SC.ops = SC.ops[:max_ops]
        SC.emit()
        build_program.info = dict(n_ops=len(SC.ops), n_sems=SC.n_sems, max_cnt=SC.max_cnt, n_waits=SC.n_waits)
    return nc


def host_consts():
    pos = np.arange(S, dtype=np.float32)

    def tables(dim):
        inv = (np.float32(THETA) ** (-np.arange(0, dim, 2, dtype=np.float32) / np.float32(dim))).astype(np.float32)
        ang = (pos[:, None] * inv[None, :]).astype(np.float32)
        ang = np.concatenate([ang, ang], axis=-1)
        cos = np.cos(ang).astype(np.float32)
        sin = np.sin(ang).astype(np.float32)
        ss = sin.copy()
        ss[:, :dim // 2] = -ss[:, :dim // 2]
        tok = lambda a: np.ascontiguousarray(a.reshape(NT, 128, dim).transpose(1, 0, 2))
        return tok(cos), tok(ss)

    cosA, ssA = tables(32)
    cosB, ssB = tables(64)
    ident = np.eye(128, dtype=np.float32)
    kk = np.arange(128)
    tri = (kk[None, :] >= kk[:, None]).astype(np.float32)
    onehot = np.zeros((8, S), np.float32)
    for n in range(8):
        onehot[n, n * 256:(n + 1) * 256] = 1.0
    biasc = np.full((8, 1024), NEG_BIG, np.float32)
    for qb in range(4):
        biasc[0:qb + 1, qb * 256:(qb + 1) * 256] = 0.0
    return dict(cosA=cosA, ssA=ssA, cosB=cosB, ssB=ssB, ident=ident, tri=tri, onehot=onehot, biasc=biasc)


def make_in_maps(inputs, nseq=2, ncores=NCORES):
    f = lambda a: np.ascontiguousarray(np.asarray(a, dtype=np.float32))
    x = f(inputs["x"])
    gains = np.concatenate([f(inputs["mla_q_norm"])[0], f(inputs["mla_kv_norm"])[0]]).reshape(8, 128).T
    shared = dict(
        w_in0=f(inputs["mla_w_in"])[0], gains=np.ascontiguousarray(gains),
        w_uq=f(inputs["mla_w_uq"])[0], w_ukv=f(inputs["mla_w_ukv"])[0], w_o0=f(inputs["mla_w_o"])[0],
        w_kv=f(inputs["moba_w_kv"]), w_in1=f(inputs["moba_w_in"])[0], w_o1=f(inputs["moba_w_o"])[0],
        ln_g=f(inputs["ln_g"]), ln_b=f(inputs["ln_b"]))
    shared.update(host_consts())
    maps = []
    for c in range(ncores):
        m = dict(shared)
        m["x"] = np.ascontiguousarray(x[c * nseq:(c + 1) * nseq])
        maps.append(m)
    return maps


_NC_CACHE = {}


def kernel(**inputs):
    nseq = 2
    if "nc" not in _NC_CACHE:
        _NC_CACHE["nc"] = build_program(nseq=nseq)
    nc = _NC_CACHE["nc"]
    in_maps = make_in_maps(inputs, nseq=nseq)
    res = run_bass_kernel_spmd(nc, in_maps, core_ids=list(range(NCORES)))
    out = np.concatenate([np.asarray(r["out"], dtype=np.float32) for r in res.results], axis=0)
    return out
```
